# Optimizing a Trainium2 kernel written in Bass

```python
import math
import jax
import jax.numpy as jnp
from jax import lax
import numpy as np

D_MODEL = 2048
BATCH = 8
SEQ = 4096
DEPTH = 2
DEC_BATCH = 2
DEC_SEQ = 4096
PAST_LEN = 128

RMS_EPS = 1e-6
GRID_W = 64
N_BRANCH = 4
BRANCH_W = 512

NA_HEADS = 8
NA_HEAD_DIM = 64
NA_WIN_R = 8
NA_WIN_C = 16
NA_QBLK = 16
NA_KBLK = 32

HY_WIDTH = 512
HY_SHORT = 3
HY_POS_BANDS = 16
HY_POS_DIM = 1 + 2 * HY_POS_BANDS
HY_FILT_HIDDEN = 64
HY_FAST_DECAY = 0.3
HY_SLOW_DECAY = 1.5
HY_DECAY_TARGET = 1e-2

MLA_HEADS = 4
MLA_Q_RANK = 512
MLA_KV_RANK = 256
MLA_NOPE = 128
MLA_ROPE = 64
MLA_V = 128
ROPE_THETA = 10000.0
ATT_QBLK = 128

DN_HEADS = 4
DN_DK = 128
DN_DV = 128
DN_CONV = 3
DN_CHUNK = 64

D_FF = 4 * D_MODEL

NA_QKV_COLS = 3 * NA_HEADS * NA_HEAD_DIM
HY_IN_COLS = 3 * HY_WIDTH
DN_QKV_COLS = DN_HEADS * (2 * DN_DK + DN_DV)
DN_GATE_COLS = DN_HEADS * DN_DV
DN_AB_COLS = 4 * DN_HEADS
GATE_COLS = N_BRANCH * D_MODEL
SPLIT_SIZES = (NA_QKV_COLS, HY_IN_COLS, MLA_Q_RANK, MLA_KV_RANK, MLA_ROPE, DN_QKV_COLS, DN_GATE_COLS, DN_AB_COLS)
IN_COLS = NA_QKV_COLS + HY_IN_COLS + MLA_Q_RANK + MLA_KV_RANK + MLA_ROPE + DN_QKV_COLS + DN_GATE_COLS + DN_AB_COLS + GATE_COLS

kernel_name = 'hybrid_bidir_encoder_gated_merge'


def rms_norm(x, g):
    xf = x.astype(jnp.float32)
    y = xf * lax.rsqrt(jnp.mean(xf * xf, axis=-1, keepdims=True) + RMS_EPS)
    return (y * g.astype(jnp.float32)).astype(x.dtype)


def l2norm(x):
    return x * lax.rsqrt(jnp.sum(x * x, axis=-1, keepdims=True) + RMS_EPS)


def centred_dwconv(x, w):
    width = w.shape[0]
    pad = width // 2
    L = x.shape[1]
    xp = jnp.pad(x, ((0, 0), (pad, pad), (0, 0)))
    return sum(xp[:, i:i + L] * w[i] for i in range(width))


def rope(x):
    L = x.shape[1]
    half = x.shape[-1] // 2
    inv = ROPE_THETA ** (-jnp.arange(half, dtype=jnp.float32) / half)
    ang = jnp.arange(L, dtype=jnp.float32)[:, None] * inv[None, :]
    shape = (1, L) + (1,) * (x.ndim - 3) + (half,)
    cos = jnp.cos(ang).reshape(shape)
    sin = jnp.sin(ang).reshape(shape)
    xf = x.astype(jnp.float32)
    x1, x2 = xf[..., :half], xf[..., half:]
    return jnp.concatenate([x1 * cos - x2 * sin, x2 * cos + x1 * sin], axis=-1).astype(x.dtype)


def neighbourhood_attention(q, k, v, rpb):
    B, L, H, dh = q.shape
    rows = L // GRID_W
    wr = min(NA_WIN_R, rows)
    ncb = GRID_W // NA_QBLK
    qcol = np.arange(GRID_W).reshape(ncb, NA_QBLK)
    cstart = np.clip(qcol - NA_WIN_C // 2, 0, GRID_W - NA_WIN_C)
    kstart = np.clip(np.arange(ncb) * NA_QBLK - NA_WIN_C // 2, 0, GRID_W - NA_KBLK)
    kcol = kstart[:, None] + np.arange(NA_KBLK)[None, :]
    kc = kcol[:, None, :]
    cs = cstart[:, :, None]
    col_ok = jnp.asarray((kc >= cs) & (kc < cs + NA_WIN_C))
    dcol = np.clip(kc - qcol[:, :, None] + NA_WIN_C - 1, 0, 2 * NA_WIN_C - 2)
    rpb_c = rpb.astype(jnp.float32)[:, :, dcol]
    qg = q.reshape(B, rows, ncb, NA_QBLK, H, dh)
    kg = k.reshape(B, rows, GRID_W, H, dh)
    vg = v.reshape(B, rows, GRID_W, H, dh)
    scale = dh ** -0.5

    def row_block(r):
        rs = jnp.clip(r - wr // 2, 0, rows - wr)
        k_blk = lax.dynamic_slice_in_dim(kg, rs, wr, axis=1)[:, :, kcol]
        v_blk = lax.dynamic_slice_in_dim(vg, rs, wr, axis=1)[:, :, kcol]
        q_row = lax.dynamic_index_in_dim(qg, r, axis=1, keepdims=False)
        s = jnp.einsum('bjqhd,bijkhd->bhjqik', q_row, k_blk, preferred_element_type=jnp.float32) * scale
        drow = rs + jnp.arange(wr) - r + NA_WIN_R - 1
        s = s + jnp.transpose(rpb_c[:, drow], (0, 2, 3, 1, 4))[None]
        s = jnp.where(col_ok[:, :, None, :], s, -jnp.inf)
        p = jax.nn.softmax(s.reshape(B, H, ncb, NA_QBLK, wr * NA_KBLK), axis=-1).reshape(s.shape)
        o = jnp.einsum('bhjqik,bijkhd->bjqhd', p.astype(v.dtype), v_blk)
        return o.reshape(B, GRID_W, H, dh)

    out = lax.map(row_block, jnp.arange(rows))
    return jnp.moveaxis(out, 0, 1).reshape(B, L, H * dh)


def hyena_filters(L, w1, b1, w2, b2, w3):
    f32 = jnp.float32
    t = jnp.linspace(0.0, 1.0, L, dtype=f32)[:, None]
    w = 2.0 * math.pi * jnp.arange(L, dtype=f32)[:, None] / L
    bands = jnp.linspace(1e-4, HY_POS_BANDS - 1, HY_POS_BANDS, dtype=f32)[None, :]
    z = jnp.concatenate([t, jnp.cos(bands * w), -jnp.sin(bands * w)], axis=-1)
    hid = jnp.sin(z @ w1.astype(f32) + b1.astype(f32))
    hid = jnp.sin(hid @ w2.astype(f32) + b2.astype(f32))
    h = (hid @ w3.astype(f32)).reshape(L, 2, HY_WIDTH)
    max_decay = math.log(HY_DECAY_TARGET) / HY_FAST_DECAY
    min_decay = math.log(HY_DECAY_TARGET) / HY_SLOW_DECAY
    deltas = jnp.abs(jnp.linspace(min_decay, max_decay, HY_WIDTH, dtype=f32))
    h = h * jnp.exp(-t * deltas[None, :])[:, None, :]
    h = h / (jnp.sum(jnp.abs(h), axis=(0, 1), keepdims=True) + RMS_EPS)
    return h[:, 0], h[:, 1]


def bidir_fftconv(v, h_f, h_b):
    L = v.shape[1]
    n = 2 * L
    k_two = jnp.concatenate([h_f[:1] + h_b[:1], h_f[1:], jnp.zeros_like(h_f[:1]), h_b[:0:-1]], axis=0)
    kf = jnp.fft.rfft(k_two, n=n, axis=0)
    vf = jnp.fft.rfft(v.astype(jnp.float32), n=n, axis=1)
    y = jnp.fft.irfft(vf * kf[None], n=n, axis=1)[:, :L]
    return y.astype(v.dtype)


def hyena_mixer(u, w_short, skip, w1, b1, w2, b2, w3):
    L = u.shape[1]
    uc = centred_dwconv(u, w_short)
    x1, x2, v = jnp.split(uc, 3, axis=-1)
    h_f, h_b = hyena_filters(L, w1, b1, w2, b2, w3)
    v = v * x1
    y = bidir_fftconv(v, h_f, h_b) + v * skip
    return y * x2


def mla_mixer(c_q, c_kv, k_r, g_q, g_kv, w_uq, w_ukv):
    B, L, _ = c_q.shape
    q = (rms_norm(c_q, g_q) @ w_uq).reshape(B, L, MLA_HEADS, MLA_NOPE + MLA_ROPE)
    kv = (rms_norm(c_kv, g_kv) @ w_ukv).reshape(B, L, MLA_HEADS, MLA_NOPE + MLA_V)
    q_nope, q_rope = q[..., :MLA_NOPE], rope(q[..., MLA_NOPE:])
    k_nope, v = kv[..., :MLA_NOPE], kv[..., MLA_NOPE:]
    k_rope = rope(k_r)
    scale = (MLA_NOPE + MLA_ROPE) ** -0.5
    nblk = L // ATT_QBLK

    def blocks(t):
        return jnp.moveaxis(t.reshape((B, nblk, ATT_QBLK) + t.shape[2:]), 1, 0)

    def attend(qs):
        qn, qr = qs
        s = jnp.einsum('bqhd,bkhd->bhqk', qn, k_nope, preferred_element_type=jnp.float32)
        s = s + jnp.einsum('bqhd,bkd->bhqk', qr, k_rope, preferred_element_type=jnp.float32)
        p = jax.nn.softmax(s * scale, axis=-1)
        return jnp.einsum('bhqk,bkhd->bqhd', p.astype(v.dtype), v)

    o = lax.map(attend, (blocks(q_nope), blocks(q_rope)))
    return jnp.moveaxis(o, 0, 1).reshape(B, L, MLA_HEADS * MLA_V)


def gated_delta_chunked(q, k, v, g, beta):
    B, L, H, dk = q.shape
    dv = v.shape[-1]
    C = DN_CHUNK
    N = L // C

    def to_chunks(t):
        return jnp.moveaxis(t.reshape((B, N, C, H) + t.shape[3:]), 3, 1)

    q = to_chunks(q) * dk ** -0.5
    k = to_chunks(k)
    v = to_chunks(v)
    g = to_chunks(g)
    beta = to_chunks(beta)
    gc = jnp.cumsum(g, axis=-1)
    tri = jnp.tril(jnp.ones((C, C), dtype=bool))
    strict = jnp.tril(jnp.ones((C, C), dtype=bool), -1)
    decay = jnp.exp(jnp.where(tri, gc[..., :, None] - gc[..., None, :], -jnp.inf))
    kb = k * beta[..., None]
    a_kk = jnp.where(strict, jnp.einsum('bhncd,bhnsd->bhncs', kb, k) * decay, 0.0)
    eye = jnp.eye(C, dtype=jnp.float32)
    t_inv = lax.linalg.triangular_solve(eye + a_kk, jnp.broadcast_to(eye, a_kk.shape),
                                        left_side=True, lower=True, unit_diagonal=True)
    u = t_inv @ (v * beta[..., None])
    w = t_inv @ (kb * jnp.exp(gc)[..., None])
    a_qk = jnp.where(tri, jnp.einsum('bhncd,bhnsd->bhncs', q, k) * decay, 0.0)
    q_dec = q * jnp.exp(gc)[..., None]
    k_dec = k * jnp.exp(gc[..., -1:] - gc)[..., None]
    g_last = jnp.exp(gc[..., -1])

    def step(S, xs):
        u_i, w_i, qd_i, a_i, kd_i, gl_i = xs
        v_new = u_i - w_i @ S
        o_i = qd_i @ S + a_i @ v_new
        S = S * gl_i[..., None, None] + jnp.swapaxes(kd_i, -1, -2) @ v_new
        return S, o_i

    xs = tuple(jnp.moveaxis(t, 2, 0) for t in (u, w, q_dec, a_qk, k_dec, g_last))
    s0 = jnp.zeros((B, H, dk, dv), jnp.float32)
    _, o = lax.scan(step, s0, xs)
    return jnp.transpose(o, (1, 0, 3, 2, 4)).reshape(B, L, H, dv)


def deltanet_mixer(qkv, gate, ab, w_conv, a_log, dt_bias, g_norm):
    B, L, _ = qkv.shape
    f32 = jnp.float32
    qkv = jax.nn.silu(centred_dwconv(qkv, w_conv)).astype(f32)
    q, k, v = jnp.split(qkv, [DN_HEADS * DN_DK, 2 * DN_HEADS * DN_DK], axis=-1)
    q = l2norm(q.reshape(B, L, DN_HEADS, DN_DK))
    k = l2norm(k.reshape(B, L, DN_HEADS, DN_DK))
    v = v.reshape(B, L, DN_HEADS, DN_DV)
    ab = ab.astype(f32).reshape(B, L, 2, 2, DN_HEADS)
    g = -jnp.exp(a_log.astype(f32)) * jax.nn.softplus(ab[:, :, :, 0] + dt_bias.astype(f32))
    beta = jax.nn.sigmoid(ab[:, :, :, 1])
    o_fwd = gated_delta_chunked(q, k, v, g[:, :, 0], beta[:, :, 0])
    flip = lambda t: jnp.flip(t, axis=1)
    o_bwd = flip(gated_delta_chunked(flip(q), flip(k), flip(v), flip(g[:, :, 1]), flip(beta[:, :, 1])))
    o = rms_norm(o_fwd + o_bwd, g_norm) * jax.nn.silu(gate.astype(f32).reshape(B, L, DN_HEADS, DN_DV))
    return o.reshape(B, L, DN_HEADS * DN_DV).astype(gate.dtype)


def mixer_block(h, w_in, na_rpb, hy_short, hy_skip, hy_w1, hy_b1, hy_w2, hy_b2, hy_w3,
                mla_g_q, mla_g_kv, mla_w_uq, mla_w_ukv, dn_conv, dn_a_log, dn_dt_bias, dn_g_norm,
                w_branch, w_out):
    B, L, _ = h.shape
    z = h @ w_in
    cuts = [int(c) for c in np.cumsum(SPLIT_SIZES)]
    na_qkv, hy_u, c_q, c_kv, k_r, dn_qkv, dn_gate, dn_ab, gate_logits = jnp.split(z, cuts, axis=-1)
    na_qkv = na_qkv.reshape(B, L, 3, NA_HEADS, NA_HEAD_DIM)
    br_a = neighbourhood_attention(na_qkv[:, :, 0], na_qkv[:, :, 1], na_qkv[:, :, 2], na_rpb)
    br_b = hyena_mixer(hy_u, hy_short, hy_skip, hy_w1, hy_b1, hy_w2, hy_b2, hy_w3)
    br_c = mla_mixer(c_q, c_kv, k_r, mla_g_q, mla_g_kv, mla_w_uq, mla_w_ukv)
    br_d = deltanet_mixer(dn_qkv, dn_gate, dn_ab, dn_conv, dn_a_log, dn_dt_bias, dn_g_norm)
    gates = jax.nn.sigmoid(gate_logits.reshape(B, L, N_BRANCH, D_MODEL))
    branches = (br_a, br_b, br_c, br_d)
    merged = sum(gates[:, :, n] * (branches[n] @ w_branch[n]) for n in range(N_BRANCH))
    return merged @ w_out


def setup_inputs(seed: int = 0) -> dict:
    key = jax.random.key(seed)
    ks = iter(jax.random.split(key, 40))
    f32 = jnp.float32

    def nrm(shape, scale):
        return jax.random.normal(next(ks), shape, f32) * scale

    def gain(shape):
        return 1.0 + nrm(shape, 0.01)

    dt = jnp.exp(jax.random.uniform(next(ks), (DEPTH, 2, DN_HEADS), f32)
                 * (math.log(0.1) - math.log(1e-3)) + math.log(1e-3))
    return {
        'x_prompt': nrm((BATCH, SEQ, D_MODEL), 1.0),
        'x_sample': nrm((DEC_BATCH, DEC_SEQ, D_MODEL), 1.0),
        'norm_mix': gain((DEPTH, D_MODEL)),
        'w_in': nrm((DEPTH, D_MODEL, IN_COLS), D_MODEL ** -0.5),
        'na_rpb': nrm((DEPTH, NA_HEADS, 2 * NA_WIN_R - 1, 2 * NA_WIN_C - 1), 0.1),
        'hy_short': nrm((DEPTH, HY_SHORT, HY_IN_COLS), HY_SHORT ** -0.5),
        'hy_skip': nrm((DEPTH, HY_WIDTH), 1.0),
        'hy_w1': nrm((DEPTH, HY_POS_DIM, HY_FILT_HIDDEN), 1.0),
        'hy_b1': nrm((DEPTH, HY_FILT_HIDDEN), 0.1),
        'hy_w2': nrm((DEPTH, HY_FILT_HIDDEN, HY_FILT_HIDDEN), HY_FILT_HIDDEN ** -0.5),
        'hy_b2': nrm((DEPTH, HY_FILT_HIDDEN), 0.1),
        'hy_w3': nrm((DEPTH, HY_FILT_HIDDEN, 2 * HY_WIDTH), HY_FILT_HIDDEN ** -0.5),
        'mla_g_q': gain((DEPTH, MLA_Q_RANK)),
        'mla_g_kv': gain((DEPTH, MLA_KV_RANK)),
        'mla_w_uq': nrm((DEPTH, MLA_Q_RANK, MLA_HEADS * (MLA_NOPE + MLA_ROPE)), MLA_Q_RANK ** -0.5),
        'mla_w_ukv': nrm((DEPTH, MLA_KV_RANK, MLA_HEADS * (MLA_NOPE + MLA_V)), MLA_KV_RANK ** -0.5),
        'dn_conv': nrm((DEPTH, DN_CONV, DN_QKV_COLS), DN_CONV ** -0.5),
        'dn_a_log': jnp.log(jax.random.uniform(next(ks), (DEPTH, 2, DN_HEADS), f32, minval=1.0, maxval=16.0)),
        'dn_dt_bias': dt + jnp.log(-jnp.expm1(-dt)),
        'dn_g_norm': gain((DEPTH, DN_DV)),
        'w_branch': nrm((DEPTH, N_BRANCH, BRANCH_W, D_MODEL), BRANCH_W ** -0.5),
        'w_out': nrm((DEPTH, D_MODEL, D_MODEL), D_MODEL ** -0.5),
        'norm_mlp': gain((DEPTH, D_MODEL)),
        'w_up': nrm((DEPTH, D_MODEL, D_FF), D_MODEL ** -0.5),
        'w_down': nrm((DEPTH, D_FF, D_MODEL), D_FF ** -0.5),
        'norm_final': gain((D_MODEL,)),
    }


def reference(x_prompt, x_sample, norm_mix, w_in, na_rpb, hy_short, hy_skip, hy_w1, hy_b1, hy_w2, hy_b2, hy_w3,
              mla_g_q, mla_g_kv, mla_w_uq, mla_w_ukv, dn_conv, dn_a_log, dn_dt_bias, dn_g_norm,
              w_branch, w_out, norm_mlp, w_up, w_down, norm_final):
    def trunk(x):
        for l in range(DEPTH):
            h = rms_norm(x, norm_mix[l])
            x = x + mixer_block(h, w_in[l], na_rpb[l], hy_short[l], hy_skip[l], hy_w1[l], hy_b1[l], hy_w2[l],
                                hy_b2[l], hy_w3[l], mla_g_q[l], mla_g_kv[l], mla_w_uq[l], mla_w_ukv[l],
                                dn_conv[l], dn_a_log[l], dn_dt_bias[l], dn_g_norm[l], w_branch[l], w_out[l])
            h = rms_norm(x, norm_mlp[l])
            x = x + jnp.square(jax.nn.relu(h @ w_up[l])) @ w_down[l]
        return rms_norm(x, norm_final)

    y_prompt = trunk(x_prompt)
    y_sample = trunk(x_sample)
    return (y_prompt, y_sample)
```

```python
import contextlib
import math
import numpy as np
import ml_dtypes
import concourse.bass as bass
import concourse.mybir as mybir
from concourse.bass_utils import run_bass_kernel_spmd

F32 = mybir.dt.float32
BF16 = mybir.dt.bfloat16
AF = mybir.ActivationFunctionType
ALU = mybir.AluOpType
AX = mybir.AxisListType

L = 4096
D = 2048
DEPTH = 2
T = 512
NT = L // T
RMS_EPS = 1e-6
IN_COLS = 14160
D_FF = 8192
NFM = 3456
NTM = 2560
R_NAQ, R_NAK, R_HY, R_CQ, R_CKV, R_KR, R_KRS = 0, 512, 1024, 2560, 3072, 3328, 3392
C_NAV, C_DNQ, C_DNK, C_DNV, C_DNG = 0, 512, 1024, 1536, 2048

ENGS = ['pe', 'act', 'dve', 'pool', 'sp']
DMA_POOL = 8
EPOCH = 20000


class Prog:
    def __init__(self, nc):
        self.nc = nc
        self.ops = {e: [] for e in ENGS}
        self.res_w = {}
        self.res_r = {}
        self.pending_barrier = {e: set() for e in ENGS}

    def op(self, eng, fn, reads=(), writes=(), dma=False, acc=False):
        idx = len(self.ops[eng])
        deps = set()
        for r in reads:
            w = self.res_w.get(r)
            if w is not None:
                deps.add(w)
            if isinstance(r, tuple) and r[0] == 'ps':
                for rd in self.res_r.get(r, ()):
                    if rd[0] != eng:
                        deps.add(rd)
        for r in writes:
            w = self.res_w.get(r)
            if w is not None and not (acc and w[0] == eng):
                deps.add(w)
            for rd in self.res_r.get(r, ()):
                deps.add(rd)
        if self.pending_barrier[eng]:
            deps |= self.pending_barrier[eng]
            self.pending_barrier[eng] = set()
        deps.discard((eng, idx))
        if eng == 'pe':
            deps = {d for d in deps if d[0] != 'pe'}
        self.ops[eng].append(dict(fn=fn, deps=deps, dma=dma, sig=False))
        for r in reads:
            self.res_r.setdefault(r, []).append((eng, idx))
        for r in writes:
            self.res_w[r] = (eng, idx)
            self.res_r[r] = []
        return (eng, idx)

    def dma(self, q, out, in_, reads=(), writes=()):
        return self.op(q, lambda e: e.dma_start(out=out, in_=in_), reads, writes, dma=True)

    def barrier(self):
        deps = set()
        for e in ENGS:
            n = len(self.ops[e])
            if n == 0:
                continue
            if e in ('sp', 'pool'):
                k = 0
                i = n - 1
                while i >= 0 and k < DMA_POOL:
                    if self.ops[e][i]['dma']:
                        k += 1
                    deps.add((e, i))
                    i -= 1
            else:
                deps.add((e, n - 1))
        for e in ENGS:
            self.pending_barrier[e] |= deps
        self.res_w = {}
        self.res_r = {}

    def emit(self, final_waits=()):
        nc = self.nc
        ops = self.ops
        for e in ENGS:
            for o in ops[e]:
                for (e2, i2) in o['deps']:
                    ops[e2][i2]['sig'] = True
        for (e2, i2) in final_waits:
            ops[e2][i2]['sig'] = True
        sem_keys = set()
        for e in ENGS:
            cnt = 0
            dcnt = 0
            for o in ops[e]:
                if o['dma']:
                    k = dcnt
                    dcnt += 1
                    key = ('d', e, k % DMA_POOL)
                    o['signal'] = (key, 16 * (k // DMA_POOL + 1))
                    o['prewait'] = (key, 16 * (k // DMA_POOL)) if k >= DMA_POOL else None
                    sem_keys.add(key)
                elif o['sig']:
                    key = ('c', e, cnt // EPOCH)
                    o['signal'] = (key, cnt % EPOCH + 1)
                    cnt += 1
                    sem_keys.add(key)
                else:
                    o['signal'] = None
        with contextlib.ExitStack() as es:
            sems = {}
            for key in sorted(sem_keys):
                sems[key] = es.enter_context(nc.semaphore("s_%s_%s_%d" % key))
            block = es.enter_context(nc.Block())

            def make(ename):
                def body(eng):
                    waited = {}
                    for o in ops[ename]:
                        need = {}
                        if o['dma'] and o['prewait'] is not None:
                            k, v = o['prewait']
                            need[k] = v
                        for (e2, i2) in o['deps']:
                            k, v = ops[e2][i2]['signal']
                            if v > need.get(k, 0):
                                need[k] = v
                        for k, v in need.items():
                            if waited.get(k, 0) >= v:
                                continue
                            eng.wait_ge(sems[k], v)
                            waited[k] = v
                        ins = o['fn'](eng)
                        if o['signal'] is not None:
                            k, v = o['signal']
                            ins.then_inc(sems[k], 16 if o['dma'] else 1)
                    if ename == 'sp':
                        for (e2, i2) in final_waits:
                            k, v = ops[e2][i2]['signal']
                            if waited.get(k, 0) < v:
                                eng.wait_ge(sems[k], v)
                                waited[k] = v
                return body

            block.tensor(make('pe'))
            block.scalar(make('act'))
            block.vector(make('dve'))
            block.gpsimd(make('pool'))
            block.sync(make('sp'))


class Rot:
    def __init__(self, name, tiles, keys=None):
        self.name = name
        self.tiles = tiles
        self.keys = keys
        self.i = 0

    def next(self):
        j = self.i % len(self.tiles)
        self.i += 1
        return self.tiles[j], (self.keys[j] if self.keys else (self.name, j))


_CONST = {}


def _bf16(a):
    return np.ascontiguousarray(a.astype(ml_dtypes.bfloat16))


def na_tables():
    cases = []
    for j in range(5):
        cases.append((2, j))
    for m in (0, 1):
        for kt in range(4):
            cases.append((m, kt))
    for m in (30, 31):
        for kt in range(28, 32):
            cases.append((m, kt))
    mask = np.zeros((21, 128, 128), np.float32)
    drow = np.zeros((21, 128, 128), np.int64)
    dcol = np.zeros((21, 128, 128), np.int64)
    a = np.arange(128) // 64
    c = np.arange(128) % 64
    for ci, (m, kt) in enumerate(cases):
        kr = (2 * kt + a)[:, None]
        kc = c[:, None]
        qr = (2 * m + a)[None, :]
        qc = c[None, :]
        rs = np.clip(qr - 4, 0, 56)
        cs = np.clip(qc - 8, 0, 48)
        ok = (kr >= rs) & (kr < rs + 8) & (kc >= cs) & (kc < cs + 16)
        mask[ci] = ok
        drow[ci] = np.clip(kr - qr + 7, 0, 14)
        dcol[ci] = np.clip(kc - qc + 15, 0, 30)
    return mask, drow, dcol


def na_case(m, kt):
    if 2 <= m <= 29:
        return kt - (m - 2)
    if m == 0:
        return 5 + kt
    if m == 1:
        return 9 + kt
    if m == 30:
        return 13 + (kt - 28)
    return 17 + (kt - 28)


def na_kts(m):
    if 2 <= m <= 29:
        return list(range(m - 2, m + 3))
    if m < 2:
        return [0, 1, 2, 3]
    return [28, 29, 30, 31]


CP = {}


def _cp_layout():
    off = 0
    for name, n in [('ident', 128), ('ones', 128), ('maskB0', 128), ('maskBi0', 128), ('maskB1', 128), ('maskBi1', 128),
                    ('tri0', 128), ('tri1', 128), ('onesblk', 128), ('sel0', 128), ('sel1', 128),
                    ('negt', 32), ('deltas', 512)]:
        CP[name] = (off, n)
        off += n
    return off


NCP = _cp_layout()


def host_constants():
    if _CONST:
        return _CONST
    f32 = np.float32
    cp = np.zeros((128, NCP), f32)

    def put(name, arr):
        o, n = CP[name]
        cp[:arr.shape[0], o:o + n] = arr
    put('ident', np.eye(128, dtype=f32))
    put('ones', np.ones((128, 128), f32))
    blk = (np.arange(128)[:, None] // 64) == (np.arange(128)[None, :] // 64)
    s_ = (np.arange(128) % 64)[:, None]
    c_ = (np.arange(128) % 64)[None, :]
    put('maskB0', (blk & (s_ < c_)).astype(f32))
    put('maskBi0', (blk & (s_ <= c_)).astype(f32))
    put('maskB1', (blk & (s_ > c_)).astype(f32))
    put('maskBi1', (blk & (s_ >= c_)).astype(f32))
    put('tri0', (blk & (s_ <= c_)).astype(f32))
    put('tri1', (blk & (s_ >= c_)).astype(f32))
    put('onesblk', blk.astype(f32))
    sel0 = np.zeros((128, 128), f32)
    sel0[:64] = 1
    sel1 = np.zeros((128, 128), f32)
    sel1[64:] = 1
    put('sel0', sel0)
    put('sel1', sel1)
    t = np.linspace(0.0, 1.0, L, dtype=f32)
    put('negt', (-t).reshape(32, 128).T)
    max_decay = math.log(1e-2) / 0.3
    min_decay = math.log(1e-2) / 1.5
    deltas = np.abs(np.linspace(min_decay, max_decay, 512, dtype=f32))
    put('deltas', np.tile(deltas[None, :], (128, 1)))
    mask, drow, dcol = na_tables()
    _CONST['namask'] = np.ascontiguousarray(mask.transpose(1, 0, 2).reshape(128, 21 * 128))
    _CONST['cp'] = cp
    _CONST['na_idx'] = (drow, dcol)
    inv = (10000.0 ** (-np.arange(32, dtype=f32) / 32)).astype(f32)
    ang = (np.arange(L, dtype=f32)[:, None] * inv[None, :]).astype(f32)
    cos = np.cos(ang).T.astype(f32)
    sin = np.sin(ang).T.astype(f32)
    _CONST['rope'] = np.ascontiguousarray(np.stack([np.concatenate([cos, cos], 0), np.concatenate([-sin, sin], 0)], 0))
    w = (2.0 * np.pi * np.arange(L, dtype=f32) / L).astype(f32)[:, None]
    bands = np.linspace(1e-4, 15, 16, dtype=f32)[None, :]
    z = np.concatenate([t[:, None], np.cos(bands * w), -np.sin(bands * w)], axis=-1).astype(f32)
    _CONST['hyz'] = np.ascontiguousarray(z.T)
    n = 2 * L
    tt = np.arange(L, dtype=np.int64)
    R = np.arange(n, dtype=np.int64)
    f = np.where(R <= L, R, R - L)
    ph = (tt[:, None] * f[None, :]) % n
    angm = 2.0 * np.pi * ph.astype(np.float64) / n
    fwd = np.where((R <= L)[None, :], np.cos(angm), np.sin(angm))
    fwd[:, L] = np.cos(np.pi * tt)
    scale = np.full(n, 2.0 / n)
    scale[0] = 1.0 / n
    scale[L] = 1.0 / n
    inv_m = (fwd * scale[None, :]).T
    fw = fwd.reshape(32, 128, 64, 128).transpose(2, 1, 0, 3)
    _CONST['dftf'] = _bf16(fw)
    _CONST['dftny'] = _bf16(fwd[:, L].reshape(32, 128).T)
    iv = inv_m.reshape(4, 16, 128, 8, 512).transpose(3, 0, 2, 1, 4)
    _CONST['dfti'] = _bf16(iv)
    return _CONST


PL = {}


def _pl_layout():
    off = 0
    for name, n in [('g_mix', 16), ('g_mlp', 16), ('g_q', 4), ('g_kv', 2), ('hy_short', 36), ('hy_skip', 4),
                    ('hy_w1', 64), ('hy_b1', 1), ('hy_w2', 64), ('hy_b2', 1), ('hy_w3', 1024),
                    ('dn_alog', 8), ('dn_dtb', 8), ('dn_gn', 512)]:
        PL[name] = (off, n)
        off += n
    return off


NPL = _pl_layout()


def pack_layer_params(inp, l):
    f32 = np.float32
    pl = np.zeros((128, NPL), f32)

    def put(name, arr):
        o, n = PL[name]
        pl[:arr.shape[0], o:o + n] = arr
    put('g_mix', inp['norm_mix'][l].reshape(16, 128).T)
    put('g_mlp', inp['norm_mlp'][l].reshape(16, 128).T)
    put('g_q', inp['mla_g_q'][l].reshape(4, 128).T)
    put('g_kv', inp['mla_g_kv'][l].reshape(2, 128).T)
    put('hy_short', inp['hy_short'][l].reshape(3, 12, 128).transpose(2, 0, 1).reshape(128, 36))
    put('hy_skip', inp['hy_skip'][l].reshape(4, 128).T)
    put('hy_w1', inp['hy_w1'][l])
    put('hy_b1', inp['hy_b1'][l][:, None])
    put('hy_w2', inp['hy_w2'][l])
    put('hy_b2', inp['hy_b2'][l][:, None])
    put('hy_w3', inp['hy_w3'][l])
    put('dn_alog', np.tile(inp['dn_a_log'][l].reshape(1, 8), (128, 1)))
    put('dn_dtb', np.tile(inp['dn_dt_bias'][l].reshape(1, 8), (128, 1)))
    put('dn_gn', np.tile(inp['dn_g_norm'][l].reshape(1, 128), (128, 4)))
    return pl


class Builder:
    def __init__(self, nseq=2, depth=DEPTH, debug=False, phases=None):
        self.nseq = nseq
        self.depth = depth
        self.debug = debug
        self.phases = phases
        self.nc = bass.Bass("TRN2", target_bir_lowering=False)
        self.P = Prog(self.nc)
        self.es = contextlib.ExitStack()
        self.finals = []

    def din(self, name, shape, dt=F32):
        return self.nc.dram_tensor(name, list(shape), dt, kind="ExternalInput").ap()

    def dout(self, name, shape, dt=F32):
        return self.nc.dram_tensor(name, list(shape), dt, kind="ExternalOutput").ap()

    def dscr(self, name, shape, dt, dbg=False):
        kind = "ExternalOutput" if (dbg and self.debug) else "Internal"
        return self.nc.dram_tensor(name, list(shape), dt, kind=kind).ap()

    def carve(self, nbytes_per_part):
        n = (nbytes_per_part + 3) // 4
        o = self.aoff
        self.aoff += n
        assert self.aoff <= self.asize, ("arena overflow", self.aoff * 4)
        return o

    def tile(self, shape, dt):
        n = int(np.prod(shape[1:]))
        esz = 2 if dt == BF16 else 4
        o = self.carve(n * esz)
        v = self.arena[:, o:o + (n * esz + 3) // 4]
        if dt != F32:
            v = v.bitcast(dt)
            v = v[:, 0:n]
        if len(shape) == 3:
            v = v.rearrange("p (a b) -> p a b", a=shape[1])
        elif len(shape) == 4:
            v = v.rearrange("p (a b c) -> p a b c", a=shape[1], b=shape[2])
        return v

    def arena_reset(self):
        self.aoff = self.abase

    def mm(self, out, lhsT, rhs, start, stop, reads, writes):
        self.P.op('pe', lambda e: e.matmul(out, lhsT=lhsT, rhs=rhs, start=start, stop=stop), reads, writes, acc=not start)

    def tr(self, out, in_, ident, reads, writes):
        self.P.op('pe', lambda e: e.transpose(out, in_, ident), reads, writes)

    def act(self, out, in_, func, reads, writes, bias=None, scale=None):
        kw = {}
        if bias is not None:
            kw['bias'] = bias
        if scale is not None:
            kw['scale'] = scale
        self.P.op('act', lambda e: e.activation(out=out, in_=in_, func=func, **kw), reads, writes)

    def tt(self, eng, out, in0, in1, op, reads, writes):
        self.P.op(eng, lambda e: e.tensor_tensor(out=out, in0=in0, in1=in1, op=op), reads, writes)

    def ts(self, eng, out, in0, s1, op0, reads, writes, s2=None, op1=None):
        if op1 is None:
            self.P.op(eng, lambda e: e.tensor_scalar(out=out, in0=in0, scalar1=s1, scalar2=None, op0=op0), reads, writes)
        else:
            self.P.op(eng, lambda e: e.tensor_scalar(out=out, in0=in0, scalar1=s1, scalar2=s2, op0=op0, op1=op1), reads, writes)

    def stt(self, out, in0, scalar, in1, op0, op1, reads, writes):
        self.P.op('dve', lambda e: e.scalar_tensor_tensor(out=out, in0=in0, scalar=scalar, in1=in1, op0=op0, op1=op1), reads, writes)

    def cp(self, eng, out, in_, reads, writes):
        if eng == 'act':
            self.P.op('act', lambda e: e.copy(out=out, in_=in_), reads, writes)
        else:
            self.P.op(eng, lambda e: e.tensor_copy(out=out, in_=in_), reads, writes)

    def reduce_x(self, out, in_, reads, writes):
        self.P.op('dve', lambda e: e.tensor_reduce(out=out, in_=in_, axis=AX.X, op=ALU.add), reads, writes)

    def recip(self, out, in_, reads, writes):
        self.P.op('dve', lambda e: e.reciprocal(out=out, in_=in_), reads, writes)

    def psrot(self, name, idxs, width=None):
        idxs = list(idxs)
        tiles = [self.ps[i][:] if width is None else self.ps[i][:, 0:width] for i in idxs]
        return Rot(name, tiles, [('ps', i) for i in idxs])

    def C(self, name):
        o, n = CP[name]
        return self.cpt[:, o:o + n]

    def PLv(self, name):
        o, n = PL[name]
        return self.plt[:, o:o + n]

    def build(self):
        nc, P, es = self.nc, self.P, self.es
        S = self.nseq
        dbg = self.debug
        self.xT_in = self.din("xT", [S, D, L])
        self.w_in = self.din("w_in", [DEPTH, D, IN_COLS])
        self.w_uq = self.din("mla_w_uq", [DEPTH, 512, 768])
        self.w_ukv = self.din("mla_w_ukv", [DEPTH, 256, 1024])
        self.w_branch = self.din("w_branch", [DEPTH, 4, 512, D])
        self.w_out = self.din("w_out", [DEPTH, D, D])
        self.w_up = self.din("w_up", [DEPTH, D, D_FF])
        self.w_down = self.din("w_down", [DEPTH, D_FF, D])
        self.pl_in = self.din("pl", [DEPTH, 128, NPL])
        self.gfin_in = self.din("gfin", [128, 16])
        self.cp_in = self.din("cp", [128, NCP])
        self.rope_in = self.din("rope", [2, 64, L])
        self.hyz_in = self.din("hyz", [33, L])
        self.dftf_in = self.din("dftf", [64, 128, 32, 128], BF16)
        self.dftny_in = self.din("dftny", [128, 32], BF16)
        self.dfti_in = self.din("dfti", [8, 4, 128, 16, 512], BF16)
        self.rpbT_in = self.din("rpbT", [DEPTH, 8, 128, 21 * 128])
        self.namask_in = self.din("namask", [128, 21 * 128])
        self.dnconv_in = self.din("dn_conv", [DEPTH, 3, 1536])
        self.yT = self.dout("yT", [S, D, L])
        self.xT_mid = self.dscr("xT_mid", [S, D, L], F32, dbg)
        self.hT = self.dscr("hT", [S, D, L], BF16, dbg)
        self.zF = self.dscr("zF", [S, NFM, L], BF16, dbg)
        self.zT = self.dscr("zT", [S, L, NTM], BF16, dbg)
        self.zAB = self.dscr("zAB", [S, L, 16], F32, dbg)
        self.brT = self.dscr("brT", [S, 4, 512, L], BF16, dbg)
        self.wA = self.dscr("wA", [D, 6144], BF16)
        self.wAB = self.dscr("wAB", [D, 16], BF16)
        self.wG = self.dscr("wG", [D, 8192], BF16)
        self.wB = self.dscr("wB", [4, 512, D], BF16)
        self.wO = self.dscr("wO", [D, D], BF16)
        self.wU = self.dscr("wU", [D, D_FF], BF16)
        self.wD = self.dscr("wD", [D_FF, D], BF16)
        self.wUQ = self.dscr("wUQ", [512, 4, 256], BF16)
        self.wUKV = self.dscr("wUKV", [256, 1024], BF16)
        self.mqn = self.dscr("mqn", [S, 4, 128, L], BF16, dbg)
        self.mqr = self.dscr("mqr", [S, 4, 64, L], BF16, dbg)
        self.mkn = self.dscr("mkn", [S, 4, 128, L], BF16, dbg)
        self.mkr = self.dscr("mkr", [S, 64, L], BF16, dbg)
        self.mv = self.dscr("mv", [S, L, 512], BF16, dbg)
        self.naE = self.dscr("naE", [8, 128, 21 * 128], BF16, dbg)
        self.hyK = self.dscr("hyK", [2, L, 512], BF16, dbg)
        self.hyKf = self.dscr("hyKf", [64, 128, 512], F32, dbg)
        self.hyX2 = self.dscr("hyX2", [S, 512, L], F32, dbg)
        self.hyVX = self.dscr("hyVX", [S, 512, L], F32, dbg)
        self.dnQ = self.dscr("dnQ", [S, L, 512], BF16, dbg)
        self.dnK = self.dscr("dnK", [S, L, 512], BF16, dbg)
        self.dnV = self.dscr("dnV", [S, L, 512], BF16, dbg)
        self.dnGB = self.dscr("dnGB", [S, L, 16], F32, dbg)
        self.dnO = self.dscr("dnO", [S, 2, L, 512], F32, dbg)

        sb = lambda n, s, d: es.enter_context(nc.sbuf_tensor(n, s, d))
        self.cpt = sb("cpt", [128, NCP], F32)
        self.plt = sb("plt", [128, NPL], F32)
        self.gfin = sb("gfint", [128, 16], F32)
        self.identb = sb("identb", [128, 128], BF16)
        self.onesb = sb("onesb", [128, 128], BF16)
        self.asize = (176 * 1024) // 4
        self.arena = sb("arena", [128, self.asize], F32)
        self.abase = 0
        self.aoff = 0
        self.ps = [es.enter_context(nc.psum_tensor("ps%d" % i, [128, 512], F32)) for i in range(8)]

        P.dma('sp', self.cpt[:], self.cp_in, writes=['cpt'])
        P.dma('sp', self.gfin[:], self.gfin_in, writes=['gfin'])
        self.cp('dve', self.identb[:], self.C('ident'), ['cpt'], ['identb'])
        self.cp('dve', self.onesb[:], self.C('ones'), ['cpt'], ['onesb'])
        P.barrier()

        run = (lambda ph: self.phases is None or ph in self.phases)
        for l in range(self.depth):
            xsrc = self.xT_in if l == 0 else self.xT_mid
            last = (l == self.depth - 1)
            xdst = self.yT if last else self.xT_mid
            P.dma('sp', self.plt[:], self.pl_in[l], writes=['plt'])
            P.barrier()
            if run('W'):
                self.phase_weights(l)
                P.barrier()
            if run('F'):
                self.hy_filters(l)
                P.barrier()
                self.na_tables_dev(l)
                P.barrier()
            for s in range(S):
                if run('A'):
                    self.phase_A(s, l, xsrc)
                    P.barrier()
                if run('M'):
                    self.mla(s)
                    P.barrier()
                if run('N'):
                    self.na(s)
                    P.barrier()
                if run('H'):
                    self.hyena(s)
                    P.barrier()
                if run('D') or run('D1') or run('D2') or run('D3'):
                    self.deltanet(s, l)
                    P.barrier()
                if run('C'):
                    self.phase_C(s, l, xsrc, xdst, last)
                    P.barrier()
        P.emit(final_waits=self.finals)
        return nc

    def phase_weights(self, l):
        P = self.P
        w = self.w_in[l]

        def cast(dst, src):
            P.dma('pool', dst, src)
        cast(self.wA[:, 0:512], w[:, 0:512])
        cast(self.wA[:, 512:1024], w[:, 512:1024])
        for j in range(3):
            cast(self.wA[:, 1024 + j * 512:1536 + j * 512], w[:, 1536 + j * 512:2048 + j * 512])
        cast(self.wA[:, 2560:3072], w[:, 3072:3584])
        cast(self.wA[:, 3072:3328], w[:, 3584:3840])
        cast(self.wA[:, 3328:3392], w[:, 3840:3904])
        cast(self.wA[:, 3392:3424], w[:, 3872:3904])
        cast(self.wA[:, 3424:3456], w[:, 3840:3872])
        cast(self.wA[:, 3584:4096], w[:, 1024:1536])
        for j in range(3):
            cast(self.wA[:, 4096 + j * 512:4608 + j * 512], w[:, 3904 + j * 512:4416 + j * 512])
        cast(self.wA[:, 5632:6144], w[:, 5440:5952])
        cast(self.wAB[:, :], w[:, 5952:5968])
        for j in range(16):
            cast(self.wG[:, j * 512:(j + 1) * 512], w[:, 5968 + j * 512:5968 + (j + 1) * 512])
        wb = self.w_branch[l].rearrange("n k d -> (n k) d")
        wbd = self.wB.rearrange("n k d -> (n k) d")
        for j in range(4):
            cast(wbd[:, j * 512:(j + 1) * 512], wb[:, j * 512:(j + 1) * 512])
            cast(self.wO[:, j * 512:(j + 1) * 512], self.w_out[l][:, j * 512:(j + 1) * 512])
        for j in range(16):
            cast(self.wU[:, j * 512:(j + 1) * 512], self.w_up[l][:, j * 512:(j + 1) * 512])
        for r in range(4):
            for j in range(4):
                cast(self.wD[r * 2048:(r + 1) * 2048, j * 512:(j + 1) * 512],
                     self.w_down[l][r * 2048:(r + 1) * 2048, j * 512:(j + 1) * 512])
        uq = self.w_uq[l].rearrange("k (h c) -> k h c", h=4)
        cast(self.wUQ[:, :, 0:192], uq)
        cast(self.wUQ[:, :, 192:224], uq[:, :, 160:192])
        cast(self.wUQ[:, :, 224:256], uq[:, :, 128:160])
        ukv = self.w_ukv[l].rearrange("k (h c) -> k h c", h=4)
        cast(self.wUKV[:, 0:512].rearrange("k (h c) -> k h c", h=4), ukv[:, :, 0:128])
        cast(self.wUKV[:, 512:1024].rearrange("k (h c) -> k h c", h=4), ukv[:, :, 128:256])

    def rms_fm(self, src_chunks, nch, nfeat, g_ap, dst, psi, tag, reads, dst_key):
        ps_ss = self.ps[psi][:]
        ssk = ('ps', psi)
        sqr = Rot(tag + 'sq', [self.tile([128, T], F32) for _ in range(2)])
        sd = self.tile([128, T], F32)
        rstd = self.tile([128, T], F32)
        for c in range(nch):
            sq, k = sqr.next()
            self.act(sq, src_chunks(c), AF.Square, reads, [k])
            self.mm(ps_ss, self.C('ones'), sq, c == 0, c == nch - 1, [k], [ssk])
        self.act(sd, ps_ss, AF.Sqrt, [ssk], [tag + 'sd'], bias=RMS_EPS, scale=1.0 / nfeat)
        self.recip(rstd, sd, [tag + 'sd'], [tag + 'rstd'])
        for c in range(nch):
            self.stt(dst[:, c, :], src_chunks(c), g_ap[:, c:c + 1], rstd, ALU.mult, ALU.mult,
                     list(reads) + [tag + 'rstd'], [dst_key])

    def phase_A(self, s, l, xsrc):
        P = self.P
        self.arena_reset()
        X32 = self.tile([128, 16, T], F32)
        H = self.tile([128, 16, T], BF16)
        wrot = Rot('Aw', [self.tile([128, 16, 512], BF16) for _ in range(2)])
        wab = self.tile([128, 16, 16], BF16)
        strot = Rot('Ast', [self.tile([128, 512], BF16) for _ in range(4)])
        stab = self.tile([128, 4, 16], F32)
        psr = self.psrot('Aps', range(1, 7))
        xv = xsrc[s].rearrange("(c p) t -> p c t", p=128)
        hv = self.hT[s].rearrange("(c p) t -> p c t", p=128)
        P.dma('sp', wab, self.wAB.rearrange("(c p) n -> p c n", p=128), writes=['Awab'])
        for i in range(NT):
            t0 = i * T
            P.dma('sp', X32, xv[:, :, t0:t0 + T], writes=['AX'])
            self.rms_fm(lambda c: X32[:, c, :], 16, D, self.PLv('g_mix'), H, 0, 'A', ['AX'], 'AH')
            P.dma('pool', hv[:, :, t0:t0 + T], H, reads=['AH'])
            ev = 0
            for g in range(7):
                nm = 4 if g < 6 else 3
                wt, wk = wrot.next()
                P.dma('sp', wt[:, :, 0:nm * 128], self.wA.rearrange("(c p) n -> p c n", p=128)[:, :, g * 512:g * 512 + nm * 128], writes=[wk])
                for m in range(nm):
                    pt, pk = psr.next()
                    for c in range(16):
                        self.mm(pt, wt[:, c, m * 128:(m + 1) * 128], H[:, c, :], c == 0, c == 15, [wk, 'AH'], [pk])
                    st, sk = strot.next()
                    self.cp('act' if ev % 2 == 0 else 'dve', st, pt, [pk], [sk])
                    ev += 1
                    r0 = g * 512 + m * 128
                    P.dma('pool', self.zF[s][r0:r0 + 128, t0:t0 + T], st, reads=[sk])
            for g in range(5):
                wt, wk = wrot.next()
                P.dma('sp', wt, self.wA.rearrange("(c p) n -> p c n", p=128)[:, :, 3584 + g * 512:3584 + (g + 1) * 512], writes=[wk])
                for j in range(4):
                    pt, pk = psr.next()
                    for c in range(16):
                        self.mm(pt, H[:, c, j * 128:(j + 1) * 128], wt[:, c, :], c == 0, c == 15, [wk, 'AH'], [pk])
                    st, sk = strot.next()
                    self.cp('act' if ev % 2 == 0 else 'dve', st, pt, [pk], [sk])
                    ev += 1
                    P.dma('pool', self.zT[s][t0 + j * 128:t0 + (j + 1) * 128, g * 512:(g + 1) * 512], st, reads=[sk])
            for j in range(4):
                pt, pk = psr.next()
                for c in range(16):
                    self.mm(pt[:, 0:16], H[:, c, j * 128:(j + 1) * 128], wab[:, c, :], c == 0, c == 15, ['Awab', 'AH'], [pk])
                self.cp('dve', stab[:, j, :], pt[:, 0:16], [pk], ['Astab'])
            P.dma('pool', self.zAB[s][t0:t0 + T, :].rearrange("(j p) n -> p j n", p=128), stab, reads=['Astab'])

    def mla(self, s):
        P = self.P
        self.arena_reset()
        wuq = self.tile([128, 4, 4, 256], BF16)
        wukv = self.tile([128, 2, 1024], BF16)
        CQ = self.tile([128, 4, T], BF16)
        CKV = self.tile([128, 2, T], BF16)
        KR = self.tile([128, T], BF16)
        KRS = self.tile([128, T], BF16)
        CQN = self.tile([128, 4, T], BF16)
        CKVN = self.tile([128, 2, T], BF16)
        ROPE = self.tile([128, 2, T], F32)
        strot = Rot('Mst', [self.tile([128, 512], BF16) for _ in range(4)])
        tmpr = Rot('Mtmp', [self.tile([128, T], F32) for _ in range(2)])
        tmp2r = Rot('Mtmp2', [self.tile([128, T], F32) for _ in range(2)])
        psr = self.psrot('Mps', range(2, 8))
        P.dma('sp', wuq, self.wUQ.rearrange("(c p) h n -> p c h n", p=128), writes=['Mwuq'])
        P.dma('sp', wukv, self.wUKV.rearrange("(c p) n -> p c n", p=128), writes=['Mwukv'])
        zf = self.zF[s]
        mark = self.aoff
        for i in range(NT):
            t0 = i * T
            self.aoff = mark
            P.dma('sp', CQ, zf[R_CQ:R_CQ + 512, t0:t0 + T].rearrange("(c p) t -> p c t", p=128), writes=['MCQ'])
            P.dma('sp', CKV, zf[R_CKV:R_CKV + 256, t0:t0 + T].rearrange("(c p) t -> p c t", p=128), writes=['MCKV'])
            P.dma('sp', KR[0:64, :], zf[R_KR:R_KR + 64, t0:t0 + T], writes=['MKR'])
            P.dma('sp', KRS[0:64, :], zf[R_KRS:R_KRS + 64, t0:t0 + T], writes=['MKRS'])
            P.dma('sp', ROPE[0:64, :, :], self.rope_in[:, :, t0:t0 + T].rearrange("a p t -> p a t"), writes=['MROPE'])
            self.rms_fm(lambda c: CQ[:, c, :], 4, 512, self.PLv('g_q'), CQN, 0, 'Mq', ['MCQ'], 'MCQN')
            self.rms_fm(lambda c: CKV[:, c, :], 2, 256, self.PLv('g_kv'), CKVN, 1, 'Mk', ['MCKV'], 'MCKVN')
            ev = 0
            for h in range(4):
                pt, pk = psr.next()
                for c in range(4):
                    self.mm(pt, wuq[:, c, h, 0:128], CQN[:, c, :], c == 0, c == 3, ['Mwuq', 'MCQN'], [pk])
                st, sk = strot.next()
                self.cp('act', st, pt, [pk], [sk])
                P.dma('pool', self.mqn[s][h][:, t0:t0 + T], st, reads=[sk])
                pr, prk = psr.next()
                for c in range(4):
                    self.mm(pr[0:64, :], wuq[:, c, h, 128:192], CQN[:, c, :], c == 0, c == 3, ['Mwuq', 'MCQN'], [prk])
                pq, pqk = psr.next()
                for c in range(4):
                    self.mm(pq[0:64, :], wuq[:, c, h, 192:256], CQN[:, c, :], c == 0, c == 3, ['Mwuq', 'MCQN'], [pqk])
                ta, tak = tmpr.next()
                tb, tbk = tmp2r.next()
                self.tt('dve', ta[0:64, :], pr[0:64, :], ROPE[0:64, 0, :], ALU.mult, [prk, 'MROPE'], [tak])
                self.tt('dve', tb[0:64, :], pq[0:64, :], ROPE[0:64, 1, :], ALU.mult, [pqk, 'MROPE'], [tbk])
                st, sk = strot.next()
                self.tt('pool', st[0:64, :], ta[0:64, :], tb[0:64, :], ALU.add, [tak, tbk], [sk])
                P.dma('pool', self.mqr[s][h][:, t0:t0 + T], st[0:64, :], reads=[sk])
                pt, pk = psr.next()
                for c in range(2):
                    self.mm(pt, wukv[:, c, h * 128:(h + 1) * 128], CKVN[:, c, :], c == 0, c == 1, ['Mwukv', 'MCKVN'], [pk])
                st, sk = strot.next()
                self.cp('act', st, pt, [pk], [sk])
                P.dma('pool', self.mkn[s][h][:, t0:t0 + T], st, reads=[sk])
            for j in range(4):
                pt, pk = psr.next()
                for c in range(2):
                    self.mm(pt, CKVN[:, c, j * 128:(j + 1) * 128], wukv[:, c, 512:1024], c == 0, c == 1, ['Mwukv', 'MCKVN'], [pk])
                st, sk = strot.next()
                self.cp('act' if j % 2 else 'dve', st, pt, [pk], [sk])
                P.dma('pool', self.mv[s][t0 + j * 128:t0 + (j + 1) * 128, :], st, reads=[sk])
            ta, tak = tmpr.next()
            tb, tbk = tmp2r.next()
            self.tt('dve', ta[0:64, :], KR[0:64, :], ROPE[0:64, 0, :], ALU.mult, ['MKR', 'MROPE'], [tak])
            self.tt('dve', tb[0:64, :], KRS[0:64, :], ROPE[0:64, 1, :], ALU.mult, ['MKRS', 'MROPE'], [tbk])
            st, sk = strot.next()
            self.tt('pool', st[0:64, :], ta[0:64, :], tb[0:64, :], ALU.add, [tak, tbk], [sk])
            P.dma('pool', self.mkr[s][:, t0:t0 + T], st[0:64, :], reads=[sk])
        P.barrier()
        self.arena_reset()
        KN = Rot('MKN', [self.tile([128, L], BF16) for _ in range(2)])
        KRP = self.tile([128, L], BF16)
        VH = Rot('MVH', [self.tile([128, 32, 128], BF16) for _ in range(2)])
        QN = Rot('MQN', [self.tile([128, T], BF16) for _ in range(2)])
        QR = Rot('MQR', [self.tile([128, T], BF16) for _ in range(2)])
        PT = Rot('MPT', [self.tile([128, T], BF16) for _ in range(3)])
        rcp = self.tile([128, T], F32)
        ost = Rot('MOST', [self.tile([128, T], BF16) for _ in range(2)])
        psS = self.psrot('MpsS', range(0, 3))
        psO = self.psrot('MpsO', range(3, 5))
        psU = self.psrot('MpsU', range(5, 7))
        scale = 192.0 ** -0.5
        self.P.op('pool', lambda e, t=KRP: e.memset(t[64:128, :], 0.0), [], ['MKRPz'])
        for qt in QR.tiles:
            self.P.op('pool', lambda e, qt=qt: e.memset(qt[64:128, :], 0.0), [], ['MQRz'])
        P.dma('sp', KRP[0:64, :], self.mkr[s], writes=['MKRP'])
        for h in range(4):
            kn, knk = KN.next()
            vh, vhk = VH.next()
            P.dma('sp', kn, self.mkn[s][h], writes=[knk])
            P.dma('sp', vh, self.mv[s][:, h * 128:(h + 1) * 128].rearrange("(k p) d -> p k d", p=128), writes=[vhk])
            for i in range(NT):
                t0 = i * T
                qn, qnk = QN.next()
                qr, qrk = QR.next()
                P.dma('sp', qn, self.mqn[s][h][:, t0:t0 + T], writes=[qnk])
                P.dma('sp', qr[0:64, :], self.mqr[s][h][:, t0:t0 + T], writes=[qrk])
                po, pok = psO.next()
                pu, puk = psU.next()

                def scores(kt):
                    pS, pSk = psS.next()
                    self.mm(pS, kn[:, kt * 128:(kt + 1) * 128], qn, True, False, [knk, qnk], [pSk])
                    self.mm(pS, KRP[:, kt * 128:(kt + 1) * 128], qr, False, True, ['MKRP', 'MKRPz', 'MQRz', qrk], [pSk])
                    pt, ptk = PT.next()
                    self.act(pt, pS, AF.Exp, [pSk], [ptk], scale=scale)
                    return pt, ptk
                cur = scores(0)
                for kt in range(32):
                    nxt = scores(kt + 1) if kt < 31 else None
                    pt, ptk = cur
                    self.mm(po, vh[:, kt, :], pt, kt == 0, kt == 31, [vhk, ptk], [pok])
                    self.mm(pu, self.onesb[:], pt, kt == 0, kt == 31, [ptk], [puk])
                    cur = nxt
                self.recip(rcp, pu, [puk], ['Mrcp'])
                o, ok_ = ost.next()
                self.tt('dve', o, po, rcp, ALU.mult, [pok, 'Mrcp'], [ok_])
                P.dma('pool', self.brT[s][2][h * 128:(h + 1) * 128, t0:t0 + T], o, reads=[ok_])

    def na_tables_dev(self, l):
        P = self.P
        self.arena_reset()
        rp = Rot('NTr', [self.tile([128, 21 * 128], F32) for _ in range(2)])
        eo = Rot('NTe', [self.tile([128, 21 * 128], BF16) for _ in range(2)])
        msk = self.tile([128, 21 * 128], F32)
        P.dma('sp', msk, self.namask_in, writes=['NTm'])
        for h in range(8):
            r, rk = rp.next()
            e, ek = eo.next()
            P.dma('sp', r, self.rpbT_in[l][h], writes=[rk])
            self.act(r, r, AF.Exp, [rk], [rk])
            self.tt('dve', e, r, msk, ALU.mult, [rk, 'NTm'], [ek])
            P.dma('pool', self.naE[h], e, reads=[ek])

    def na(self, s):
        P = self.P
        self.arena_reset()
        V = self.tile([128, 32, 512], BF16)
        QT = Rot('NQ', [self.tile([128, L], BF16) for _ in range(2)])
        KT = Rot('NK', [self.tile([128, L], BF16) for _ in range(2)])
        E = Rot('NE', [self.tile([128, 21, 128], BF16) for _ in range(2)])
        OUT = Rot('NO', [self.tile([128, L], BF16) for _ in range(2)])
        ES = Rot('NES', [self.tile([128, 128], F32) for _ in range(3)])
        PT = Rot('NPT', [self.tile([128, 128], BF16) for _ in range(3)])
        rcp = self.tile([128, 128], F32)
        psS = self.psrot('NpsS', range(0, 3), 128)
        psO = self.psrot('NpsO', range(3, 5), 128)
        psU = self.psrot('NpsU', range(5, 7), 128)
        P.dma('sp', V, self.zT[s][:, C_NAV:C_NAV + 512].rearrange("(k p) d -> p k d", p=128), writes=['NV'])
        scale = 64.0 ** -0.5
        for h in range(8):
            q, qk = QT.next()
            k, kk = KT.next()
            e, ek = E.next()
            out, outk = OUT.next()
            P.dma('sp', q[0:64, :], self.zF[s][R_NAQ + h * 64:R_NAQ + (h + 1) * 64, :], writes=[qk])
            P.dma('sp', k[0:64, :], self.zF[s][R_NAK + h * 64:R_NAK + (h + 1) * 64, :], writes=[kk])
            P.dma('sp', e, self.naE[h].rearrange("p (j c) -> p j c", j=21), writes=[ek])
            for m in range(32):
                kts = na_kts(m)
                po, pok = psO.next()
                pu, puk = psU.next()

                def scores(kt):
                    pS, pSk = psS.next()
                    self.mm(pS, k[0:64, kt * 128:(kt + 1) * 128], q[0:64, m * 128:(m + 1) * 128], True, True, [kk, qk], [pSk])
                    es_, esk = ES.next()
                    self.act(es_, pS, AF.Exp, [pSk], [esk], scale=scale)
                    pt, ptk = PT.next()
                    self.tt('dve', pt, es_, e[:, na_case(m, kt), :], ALU.mult, [esk, ek], [ptk])
                    return pt, ptk
                cur = scores(kts[0])
                for j, kt in enumerate(kts):
                    nxt = scores(kts[j + 1]) if j + 1 < len(kts) else None
                    pt, ptk = cur
                    self.mm(po[0:64, :], V[:, kt, h * 64:(h + 1) * 64], pt, j == 0, j == len(kts) - 1, ['NV', ptk], [pok])
                    self.mm(pu[0:64, :], self.onesb[:, 0:64], pt, j == 0, j == len(kts) - 1, [ptk], [puk])
                    cur = nxt
                self.recip(rcp[0:64, :], pu[0:64, :], [puk], ['Nrcp'])
                self.tt('dve', out[0:64, m * 128:(m + 1) * 128], po[0:64, :], rcp[0:64, :], ALU.mult, [pok, 'Nrcp'], [outk])
            P.dma('pool', self.brT[s][0][h * 64:(h + 1) * 64, :], out[0:64, :], reads=[outk])

    def sin_rr(self, out, psum_in, bias_ap, np_, tmp_f, tmp_i, reads, writes, tag):
        two_pi = 2.0 * math.pi
        tf = tmp_f[0:np_, :]
        ti = tmp_i[0:np_, :]
        o = out[0:np_, :]
        fk, ik = tag + 'f', tag + 'i'
        self.ts('dve', tf, psum_in, bias_ap, ALU.add, reads, [fk], s2=1.0 / two_pi, op1=ALU.mult)
        self.ts('dve', ti, tf, 32.5, ALU.add, [fk], [ik])
        self.cp('dve', o, ti, [ik], writes)
        self.stt(o, tf, 32.0, o, ALU.add, ALU.subtract, [fk] + list(writes), writes)
        self.P.op('dve', lambda e: e.tensor_single_scalar(tf, o, -0.5, ALU.is_lt), list(writes) + [fk], [fk])
        self.tt('dve', o, o, tf, ALU.add, [fk] + list(writes), writes)
        self.ts('dve', o, o, two_pi, ALU.mult, writes, writes, s2=math.pi, op1=ALU.min)
        self.ts('dve', o, o, -math.pi, ALU.max, writes, writes)
        self.act(o, o, AF.Sin, writes, writes)

    def hy_filters(self, l):
        P = self.P
        self.arena_reset()
        I32 = mybir.dt.int32
        ZP = self.tile([128, T], F32)
        H1 = self.tile([128, T], F32)
        H2 = self.tile([128, T], F32)
        tf = self.tile([128, T], F32)
        ti = self.tile([128, T], I32)
        WIN = self.tile([128, 512], F32)
        HF = self.tile([128, 512], F32)
        HB = self.tile([128, 512], F32)
        AB = self.tile([128, 512], F32)
        KS = self.tile([128, 32, 512], BF16)
        KD = self.tile([128, 32, 512], BF16)
        rinv = self.tile([128, 512], F32)
        ps_abs = self.ps[7][:]
        w1 = self.PLv('hy_w1')
        w2 = self.PLv('hy_w2')
        w3 = self.PLv('hy_w3')
        b1 = self.PLv('hy_b1')
        b2 = self.PLv('hy_b2')
        nt = self.C('negt')
        for i in range(NT):
            t0 = i * T
            P.dma('sp', ZP[0:33, :], self.hyz_in[:, t0:t0 + T], writes=['FZP'])
            self.mm(self.ps[0][0:64, :], w1[0:33, :], ZP[0:33, :], True, True, ['FZP'], [('ps', 0)])
            self.sin_rr(H1, self.ps[0][0:64, :], b1[0:64, :], 64, tf, ti, [('ps', 0)], ['FH1'], 'Fa')
            self.mm(self.ps[1][0:64, :], w2[0:64, :], H1[0:64, :], True, True, ['FH1'], [('ps', 1)])
            self.sin_rr(H2, self.ps[1][0:64, :], b2[0:64, :], 64, tf, ti, [('ps', 1)], ['FH2'], 'Fb')
            for j in range(4):
                tix = i * 4 + j
                self.mm(self.ps[2][:], H2[0:64, j * 128:(j + 1) * 128], w3[0:64, 0:512], True, True, ['FH2'], [('ps', 2)])
                self.mm(self.ps[3][:], H2[0:64, j * 128:(j + 1) * 128], w3[0:64, 512:1024], True, True, ['FH2'], [('ps', 3)])
                self.act(WIN, self.C('deltas'), AF.Exp, [], ['FWIN'], scale=nt[:, tix:tix + 1])
                self.tt('dve', HF, self.ps[2][:], WIN, ALU.mult, [('ps', 2), 'FWIN'], ['FHF'])
                self.tt('dve', HB, self.ps[3][:], WIN, ALU.mult, [('ps', 3), 'FWIN'], ['FHB'])
                self.tt('pool', KS[:, tix, :], HF, HB, ALU.add, ['FHF', 'FHB'], ['FKS'])
                self.tt('pool', KD[:, tix, :], HF, HB, ALU.subtract, ['FHF', 'FHB'], ['FKD'])
                self.act(HF, HF, AF.Abs, ['FHF', 'FKS', 'FKD'], ['FHF'])
                self.act(HB, HB, AF.Abs, ['FHB', 'FKS', 'FKD'], ['FHB'])
                self.tt('dve', AB, HF, HB, ALU.add, ['FHF', 'FHB'], ['FAB'])
                self.mm(ps_abs, self.C('ones'), AB, tix == 0, tix == 31, ['FAB'], [('ps', 7)])
        self.ts('dve', rinv, ps_abs, RMS_EPS, ALU.add, [('ps', 7)], ['Frinv'])
        self.recip(rinv, rinv, ['Frinv'], ['Frinv'])
        ftr = Rot('Fft', [self.tile([128, 32, 128], BF16) for _ in range(2)])
        ny = self.tile([128, 32], BF16)
        kst = Rot('Fkst', [self.tile([128, 512], F32) for _ in range(2)])
        psr = self.psrot('Fps', range(0, 4))
        P.dma('sp', ny, self.dftny_in, writes=['Fny'])
        for rt in range(64):
            ft, ftk = ftr.next()
            P.dma('sp', ft, self.dftf_in[rt], writes=[ftk])
            pt, pk = psr.next()
            src = KS if rt < 32 else KD
            sk = 'FKS' if rt < 32 else 'FKD'
            for c in range(32):
                self.mm(pt, ft[:, c, :], src[:, c, :], c == 0, c == 31, [ftk, sk], [pk])
            st, stk = kst.next()
            self.tt('dve', st, pt, rinv, ALU.mult, [pk, 'Frinv'], [stk])
            if rt == 32:
                pn = self.ps[4][:]
                for c in range(32):
                    self.mm(pn[0:1, :], ny[:, c:c + 1], KS[:, c, :], c == 0, c == 31, ['Fny', 'FKS'], [('ps', 4)])
                self.tt('dve', st[0:1, :], pn[0:1, :], rinv[0:1, :], ALU.mult, [('ps', 4), 'Frinv', stk], [stk])
            P.dma('pool', self.hyKf[rt], st, reads=[stk])

    def hyena(self, s):
        P = self.P
        self.arena_reset()
        VX = self.tile([128, 32, 512], BF16)
        PA = self.tile([128, 64, 512], BF16)
        mark = self.aoff
        U = Rot('HU', [self.tile([128, 12, T + 2], BF16) for _ in range(2)])
        ucr = Rot('Huc', [self.tile([128, T], F32) for _ in range(3)])
        X1 = self.tile([128, 4, T], F32)
        vxf = Rot('Hvxf', [self.tile([128, T], F32) for _ in range(2)])
        vxb = self.tile([128, 4, T], BF16)
        x2r = Rot('Hx2', [self.tile([128, T], F32) for _ in range(2)])
        wsh = self.PLv('hy_short')
        zf = self.zF[s]
        uv = zf[R_HY:R_HY + 1536, :].rearrange("(c p) t -> p c t", p=128)
        psr = self.psrot('Hps', range(0, 4))
        for i in range(NT):
            t0 = i * T
            u, uk = U.next()
            lo = max(t0 - 1, 0)
            hi = min(t0 + T + 1, L)
            if i == 0:
                self.P.op('pool', lambda e, u=u: e.memset(u[:, :, 0:1], 0.0), [], [uk])
            if i == NT - 1:
                self.P.op('pool', lambda e, u=u: e.memset(u[:, :, T + 1:T + 2], 0.0), [], [uk])
            P.dma('sp', u[:, :, lo - (t0 - 1):hi - (t0 - 1)], uv[:, :, lo:hi], writes=[uk], reads=[uk])
            for c in range(12):
                a, ak = ucr.next()
                self.act(a, u[:, c, 0:T], AF.Copy, [uk], [ak], scale=wsh[:, c:c + 1])
                self.stt(a, u[:, c, 1:T + 1], wsh[:, 12 + c:13 + c], a, ALU.mult, ALU.add, [uk, ak], [ak])
                if c < 4:
                    self.stt(X1[:, c, :], u[:, c, 2:T + 2], wsh[:, 24 + c:25 + c], a, ALU.mult, ALU.add, [uk, ak], ['HX1'])
                elif c < 8:
                    x2, x2k = x2r.next()
                    self.stt(x2, u[:, c, 2:T + 2], wsh[:, 24 + c:25 + c], a, ALU.mult, ALU.add, [uk, ak], [x2k])
                    P.dma('pool', self.hyX2[s][(c - 4) * 128:(c - 3) * 128, t0:t0 + T], x2, reads=[x2k])
                else:
                    self.stt(a, u[:, c, 2:T + 2], wsh[:, 24 + c:25 + c], a, ALU.mult, ALU.add, [uk, ak], [ak])
                    vf, vfk = vxf.next()
                    self.tt('pool', vf, a, X1[:, c - 8, :], ALU.mult, [ak, 'HX1'], [vfk])
                    self.cp('pool', vxb[:, c - 8, :], vf, [vfk], ['Hvxb'])
                    P.dma('pool', self.hyVX[s][(c - 8) * 128:(c - 7) * 128, t0:t0 + T], vf, reads=[vfk])
            for j in range(4):
                pt, pk = psr.next()
                ptb = pt.bitcast(BF16)
                for c in range(4):
                    self.tr(ptb[:, c * 128:(c + 1) * 128], vxb[:, c, j * 128:(j + 1) * 128], self.identb[:], ['Hvxb'], [pk])
                self.cp('act', VX[:, i * 4 + j, :], ptb[:, 0:512], [pk], ['HVX'])
        P.barrier()
        self.aoff = mark
        ftr = Rot('Hft', [self.tile([128, 32, 128], BF16) for _ in range(3)])
        kfr = Rot('Hkf', [self.tile([128, 2, 512], F32) for _ in range(2)])
        tm = [self.tile([128, 512], F32) for _ in range(4)]
        psr = self.psrot('Hps2', range(0, 4))
        for j in range(32):
            kf, kfk = kfr.next()
            P.dma('sp', kf[:, 0, :], self.hyKf[j], writes=[kfk])
            P.dma('sp', kf[:, 1, :], self.hyKf[j + 32], writes=[kfk], reads=[kfk])
            pcs = []
            for half in range(2):
                ft, ftk = ftr.next()
                P.dma('sp', ft, self.dftf_in[j + 32 * half], writes=[ftk])
                pt, pk = psr.next()
                for c in range(32):
                    self.mm(pt, ft[:, c, :], VX[:, c, :], c == 0, c == 31, [ftk, 'HVX'], [pk])
                pcs.append((pt, pk))
            (pc, pck), (ps_, psk) = pcs
            self.tt('dve', tm[0], pc, kf[:, 0, :], ALU.mult, [pck, kfk], ['Ht0'])
            self.tt('dve', tm[1], ps_, kf[:, 1, :], ALU.mult, [psk, kfk], ['Ht1'])
            self.tt('pool', PA[:, j, :], tm[0], tm[1], ALU.subtract, ['Ht0', 'Ht1'], ['HPA'])
            self.tt('dve', tm[2], pc, kf[:, 1, :], ALU.mult, [pck, kfk], ['Ht2'])
            self.tt('dve', tm[3], ps_, kf[:, 0, :], ALU.mult, [psk, kfk], ['Ht3'])
            self.tt('pool', PA[:, j + 32, :], tm[2], tm[3], ALU.add, ['Ht2', 'Ht3'], ['HPA'])
            if j == 0:
                self.tt('dve', PA[0:1, 0, :], pc[0:1, :], kf[0:1, 0, :], ALU.mult, [pck, kfk, 'HPA'], ['HPA2'])
                self.tt('dve', PA[0:1, 32, :], ps_[0:1, :], kf[0:1, 1, :], ALU.mult, [psk, kfk, 'HPA', 'HPA2'], ['HPA2'])
        P.barrier()
        self.aoff = mark
        ivr = Rot('Hiv', [self.tile([128, 16, 512], BF16) for _ in range(2)])
        vxr = Rot('Hvx', [self.tile([128, T], F32) for _ in range(2)])
        x2r = Rot('Hx2b', [self.tile([128, T], F32) for _ in range(2)])
        yr = Rot('Hy', [self.tile([128, T], F32) for _ in range(2)])
        yo = Rot('Hyo', [self.tile([128, T], BF16) for _ in range(2)])
        skip = self.PLv('hy_skip')
        for i in range(NT):
            t0 = i * T
            pacc = [(self.ps[(i % 2) * 4 + c][:], ('ps', (i % 2) * 4 + c)) for c in range(4)]
            for g in range(4):
                iv, ivk = ivr.next()
                P.dma('sp', iv, self.dfti_in[i][g], writes=[ivk])
                for ct in range(4):
                    pt, pk = pacc[ct]
                    for c in range(16):
                        self.mm(pt, PA[:, g * 16 + c, ct * 128:(ct + 1) * 128], iv[:, c, :],
                                g == 0 and c == 0, g == 3 and c == 15, [ivk], [pk])
            for ct in range(4):
                pt, pk = pacc[ct]
                vx, vxk = vxr.next()
                x2, x2k = x2r.next()
                P.dma('sp', vx, self.hyVX[s][ct * 128:(ct + 1) * 128, t0:t0 + T], writes=[vxk])
                P.dma('sp', x2, self.hyX2[s][ct * 128:(ct + 1) * 128, t0:t0 + T], writes=[x2k])
                y, yk = yr.next()
                self.stt(y, vx, skip[:, ct:ct + 1], pt, ALU.mult, ALU.add, [vxk, pk], [yk])
                o, ok_ = yo.next()
                self.tt('pool', o, y, x2, ALU.mult, [yk, x2k], [ok_])
                P.dma('pool', self.brT[s][1][ct * 128:(ct + 1) * 128, t0:t0 + T], o, reads=[ok_])

    def deltanet(self, s, l):
        P = self.P
        self.arena_reset()
        import os as _os
        sub = (lambda x: self.phases is None or 'D' in self.phases or x in self.phases)
        dn_steps = int(_os.environ.get('DN_STEPS', '64'))
        dn_mode = float(_os.environ.get('DN_MODE', '9'))
        A = [Rot('DA%d' % k, [self.tile([128, 1536], BF16) for _ in range(2)]) for k in range(3)]
        cf = self.tile([128, 1536], F32)
        c2 = self.tile([128, 1536], F32)
        sq = self.tile([128, 1024], F32)
        ss = self.tile([128, 8], F32)
        ob = Rot('Dob', [self.tile([128, 1536], BF16) for _ in range(2)])
        abt = self.tile([128, 16], F32)
        gb = Rot('Dgb', [self.tile([128, 16], F32) for _ in range(2)])
        gb2 = Rot('Dgb2', [self.tile([128, 16], F32) for _ in range(2)])
        nA = self.tile([128, 8], F32)
        wc = self.tile([128, 3 * 1536], F32)
        P.dma('sp', wc, self.dnconv_in[l].rearrange("k c -> (k c)").partition_broadcast(128), writes=['Dwc'])
        P.barrier()
        zt = self.zT[s]
        self.act(nA, self.PLv('dn_alog'), AF.Exp, [], ['DnA'])
        self.ts('dve', nA, nA, -1.0, ALU.mult, ['DnA'], ['DnA'])
        for j in range(32 if sub('D1') else 0):
            t0 = j * 128
            tiles = []
            for k in range(3):
                a, ak = A[k].next()
                sh = k - 1
                lo = max(t0 + sh, 0)
                hi = min(t0 + sh + 128, L)
                if lo != t0 + sh or hi != t0 + sh + 128:
                    self.P.op('pool', lambda e, a=a: e.memset(a, 0.0), [], [ak])
                P.dma('sp', a[lo - (t0 + sh):hi - (t0 + sh), :], zt[lo:hi, C_DNQ:C_DNQ + 1536], writes=[ak], reads=[ak])
                tiles.append((a, ak))
            (a0, a0k), (a1, a1k), (a2, a2k) = tiles
            self.tt('dve', c2, a0, wc[:, 0:1536], ALU.mult, [a0k], ['Dc2'])
            self.tt('dve', cf, a1, wc[:, 1536:3072], ALU.mult, [a1k], ['Dcf'])
            self.tt('dve', cf, cf, c2, ALU.add, ['Dcf', 'Dc2'], ['Dcf'])
            self.tt('dve', c2, a2, wc[:, 3072:4608], ALU.mult, [a2k, 'Dcf'], ['Dc2'])
            self.tt('dve', cf, cf, c2, ALU.add, ['Dcf', 'Dc2'], ['Dcf'])
            self.act(cf, cf, AF.Silu, ['Dcf'], ['Dcf'])
            self.tt('pool', sq, cf[:, 0:1024], cf[:, 0:1024], ALU.mult, ['Dcf'], ['Dsq'])
            self.reduce_x(ss, sq.rearrange("p (h d) -> p h d", h=8), ['Dsq'], ['Dss'])
            self.act(ss, ss, AF.Sqrt, ['Dss'], ['Dss'], bias=RMS_EPS, scale=1.0)
            self.recip(ss, ss, ['Dss'], ['Dss'])
            o, okk = ob.next()
            for hh in range(8):
                if hh < 4:
                    self.ts('dve', o[:, hh * 128:(hh + 1) * 128], cf[:, hh * 128:(hh + 1) * 128], ss[:, hh:hh + 1], ALU.mult,
                            ['Dcf', 'Dss'], [okk], s2=128.0 ** -0.5, op1=ALU.mult)
                else:
                    self.ts('dve', o[:, hh * 128:(hh + 1) * 128], cf[:, hh * 128:(hh + 1) * 128], ss[:, hh:hh + 1], ALU.mult,
                            ['Dcf', 'Dss'], [okk])
            self.cp('act', o[:, 1024:1536], cf[:, 1024:1536], ['Dcf'], [okk])
            P.dma('pool', self.dnQ[s][t0:t0 + 128, :], o[:, 0:512], reads=[okk])
            P.dma('pool', self.dnK[s][t0:t0 + 128, :], o[:, 512:1024], reads=[okk])
            P.dma('pool', self.dnV[s][t0:t0 + 128, :], o[:, 1024:1536], reads=[okk])
            P.dma('sp', abt, self.zAB[s][t0:t0 + 128, :], writes=['Dab'])
            g, gk = gb.next()
            g2, g2k = gb2.next()
            abv = abt.rearrange("p (d w h) -> p d w h", d=2, w=2)
            gv = g.rearrange("p (w d h) -> p w d h", w=2, d=2)
            dtb = self.PLv('dn_dtb').rearrange("p (d h) -> p d h", d=2)
            nAv = nA.rearrange("p (d h) -> p d h", d=2)
            self.tt('dve', gv[:, 0, :, :], abv[:, :, 0, :], dtb, ALU.add, ['Dab'], [gk])
            self.act(g[:, 0:8], g[:, 0:8], AF.Exp, [gk], [gk])
            self.act(g[:, 0:8], g[:, 0:8], AF.Ln, [gk], [gk], bias=1.0, scale=1.0)
            self.tt('dve', gv[:, 0, :, :], gv[:, 0, :, :], nAv, ALU.mult, [gk, 'DnA'], [gk])
            self.act(gv[:, 1, :, :], abv[:, :, 1, :], AF.Sigmoid, ['Dab', gk], [gk])
            for w_ in range(2):
                self.cp('dve', g2.rearrange("p (d h2 w hp) -> p w d hp h2", d=2, h2=2, w=2)[:, w_],
                        g.rearrange("p (w d hp h2) -> p w d hp h2", w=2, d=2, hp=2)[:, w_], [gk], [g2k])
            P.dma('pool', self.dnGB[s][t0:t0 + 128, :], g2, reads=[g2k])
        P.barrier()
        self.arena_reset()
        ident = self.C('ident')
        Sst = [[self.tile([128, 128], F32) for _ in range(4)] for _ in range(2)]
        S16 = [[self.tile([128, 128], BF16) for _ in range(4)] for _ in range(2)]
        for d in range(2):
            for h in range(4):
                self.P.op('pool', lambda e, t=Sst[d][h]: e.memset(t, 0.0), [], [('DS', d, h)])
                self.P.op('pool', lambda e, t=S16[d][h]: e.memset(t, 0.0), [], [('DS16', d, h)])
        NB = 2
        mk = lambda shape, dt: [self.tile(shape, dt) for _ in range(NB)]
        QK = [mk([128, 2, 128], BF16) for _ in range(3)]
        GBt = mk([128, 8], F32)
        gc = mk([128, 2], F32)
        gt = mk([128, 2], F32)
        sc = mk([128, 8], F32)
        egl = mk([128, 4], F32)
        scaled = mk([128, 5, 128], BF16)
        FMt = mk([128, 4, 128], BF16)
        Dg = mk([128, 128], F32)
        dm = mk([128, 128], F32)
        dB = mk([128, 128], F32)
        dBi = mk([128, 128], F32)
        Nm = mk([128, 128], F32)
        Nt = mk([128, 128], F32)
        Q2 = mk([128, 128], F32)
        Qt2 = mk([128, 128], F32)
        Pm = mk([128, 128], F32)
        TT = mk([128, 128], BF16)
        Aqk = mk([128, 128], BF16)
        um = mk([128, 128], F32)
        wT = mk([128, 128], BF16)
        vn = mk([128, 128], BF16)
        ot = mk([128, 128], F32)
        cnt = [0]
        psq = self.psrot('Dps', range(8))

        def P128(p):
            return p[:, 0:128]
        for step in range(dn_steps if sub('D2') else 0):
            for d in range(2):
                n = step if d == 0 else 63 - step
                t0 = n * 64
                for hp in range(2):
                    b = cnt[0] % NB
                    cnt[0] += 1
                    K = lambda nm: ('D' + nm, b)
                    q_, k_, v_ = QK[0][b], QK[1][b], QK[2][b]
                    for h2 in range(2):
                        for (dst, src, nm) in ((q_, self.dnQ, 'q'), (k_, self.dnK, 'k'), (v_, self.dnV, 'v')):
                            P.dma('sp', dst[h2 * 64:(h2 + 1) * 64, :, :],
                                  src[s][t0:t0 + 64, :].rearrange("t (hp h2 e) -> t hp h2 e", hp=2, h2=2)[:, :, h2, :],
                                  writes=[K(nm)], reads=[K(nm)])
                        gsrc = self.dnGB[s][t0:t0 + 64, :].rearrange("t (d h2 x) -> t d h2 x", d=2, h2=2)
                        P.dma('sp', GBt[b][h2 * 64:(h2 + 1) * 64, 0:4], gsrc[:, d, h2, :], writes=[K('gb')], reads=[K('gb')])
                    g_col = GBt[b][:, hp:hp + 1]
                    be_col = GBt[b][:, 2 + hp:3 + hp]
                    tri = self.C('tri%d' % d)
                    p1, p1k = psq.next()
                    self.mm(p1[:, 0:2], tri, GBt[b][:, 0:2], True, True, [K('gb')], [p1k])
                    self.mm(p1[:, 2:4], self.C('onesblk'), GBt[b][:, 0:2], True, True, [K('gb')], [p1k])
                    self.mm(p1[:, 4:6], self.C('sel0'), GBt[b][:, 0:2], True, True, [K('gb')], [p1k])
                    self.mm(p1[:, 6:8], self.C('sel1'), GBt[b][:, 0:2], True, True, [K('gb')], [p1k])
                    self.cp('dve', gc[b], p1[:, 0:2], [p1k], [K('gc')])
                    self.cp('dve', gt[b], p1[:, 2:4], [p1k], [K('gt')])
                    self.act(egl[b], p1[:, 4:8], AF.Exp, [p1k], [K('egl')])
                    gcc = gc[b][:, hp:hp + 1]
                    self.act(sc[b][:, 0:1], gcc, AF.Exp, [K('gc')], [K('sc')])
                    self.tt('dve', sc[b][:, 1:2], gt[b][:, hp:hp + 1], gcc, ALU.subtract, [K('gt'), K('gc'), K('sc')], [K('sc')])
                    self.act(sc[b][:, 1:2], sc[b][:, 1:2], AF.Exp, [K('sc')], [K('sc')])
                    self.tt('dve', sc[b][:, 2:3], sc[b][:, 0:1], be_col, ALU.mult, [K('sc'), K('gb')], [K('sc')])
                    kb, kbg, kd, qd, vb = [scaled[b][:, x, :] for x in range(5)]
                    kk_ = k_[:, hp, :]
                    self.ts('dve', kb, kk_, be_col, ALU.mult, [K('k'), K('gb')], [K('kb')])
                    self.ts('dve', kbg, kk_, sc[b][:, 2:3], ALU.mult, [K('k'), K('sc')], [K('kbg')])
                    self.ts('dve', kd, kk_, sc[b][:, 1:2], ALU.mult, [K('k'), K('sc')], [K('kd')])
                    self.ts('dve', qd, q_[:, hp, :], sc[b][:, 0:1], ALU.mult, [K('q'), K('sc')], [K('qd')])
                    self.ts('dve', vb, v_[:, hp, :], be_col, ALU.mult, [K('v'), K('gb')], [K('vb')])
                    if dn_mode < 2:
                        continue
                    p2, p2k = psq.next()
                    p2b = p2.bitcast(BF16)
                    self.tr(p2b[:, 0:128], kk_, self.identb[:], [K('k')], [p2k])
                    self.tr(p2b[:, 128:256], kb, self.identb[:], [K('kb')], [p2k])
                    self.tr(p2b[:, 256:384], q_[:, hp, :], self.identb[:], [K('q')], [p2k])
                    self.tr(p2b[:, 384:512], qd, self.identb[:], [K('qd')], [p2k])
                    fm = FMt[b]
                    self.cp('act', fm.rearrange("p a b -> p (a b)"), p2b[:, 0:512], [p2k], [K('fm')])
                    kT, kbT, qsT, qdT = [fm[:, x, :] for x in range(4)]
                    if dn_mode < 2.2:
                        continue
                    p3, p3k = psq.next()
                    self.mm(p3[:, 0:128], kT, kbT, True, True, [K('fm')], [p3k])
                    self.mm(p3[:, 128:256], kT, qsT, True, True, [K('fm')], [p3k])
                    if dn_mode < 2.4:
                        continue
                    self.ts('dve', Dg[b], ident, gcc, ALU.mult, [K('gc')], [K('Dg')])
                    self.mm(p3[:, 256:384], self.C('onesblk'), Dg[b], True, True, [K('Dg')], [p3k])
                    if dn_mode < 2.6:
                        continue
                    self.ts('dve', dm[b], p3[:, 256:384], gcc, ALU.subtract, [p3k, K('gc')], [K('dm')], s2=0.0, op1=ALU.min)
                    self.act(dm[b], dm[b], AF.Exp, [K('dm')], [K('dm')])
                    if dn_mode < 2.8:
                        continue
                    self.tt('pool', dB[b], dm[b], self.C('maskB%d' % d), ALU.mult, [K('dm')], [K('dB')])
                    self.tt('pool', dBi[b], dm[b], self.C('maskBi%d' % d), ALU.mult, [K('dm')], [K('dBi')])
                    self.stt(Nm[b], p3[:, 0:128], -1.0, dB[b], ALU.mult, ALU.mult, [p3k, K('dB')], [K('N')])
                    self.tt('dve', Aqk[b], p3[:, 128:256], dBi[b], ALU.mult, [p3k, K('dBi')], [K('Aqk')])
                    if dn_mode < 3:
                        continue
                    p4, p4k = psq.next()
                    self.tr(p4[:, 0:128], Nm[b], ident, [K('N')], [p4k])
                    self.cp('act', Nt[b], p4[:, 0:128], [p4k], [K('Nt')])
                    if dn_mode < 3.2:
                        continue
                    self.tt('pool', Pm[b], Nm[b], ident, ALU.add, [K('N')], [K('P')])
                    if dn_mode < 3.4:
                        continue
                    Q, Qk = Nm[b], K('N')
                    Qt, Qtk = Nt[b], K('Nt')
                    Qn, Qnk = Q2[b], K('Q2')
                    Qtn, Qtnk = Qt2[b], K('Qt2')
                    for it in range(5):
                        pa, pak = psq.next()
                        self.mm(pa[:, 0:128], Q, Qt, True, True, [Qk, Qtk], [pak])
                        if it < 4:
                            self.mm(pa[:, 128:256], Qt, Q, True, True, [Qk, Qtk], [pak])
                        self.cp('act', Qtn, pa[:, 0:128], [pak], [Qtnk])
                        if it < 4:
                            self.cp('dve', Qn, pa[:, 128:256], [pak], [Qnk])
                        pb_, pbk = psq.next()
                        self.mm(pb_[:, 0:128], Qtn, Pm[b], True, True, [Qtnk, K('P')], [pbk])
                        if it < 4:
                            self.tt('dve', Pm[b], Pm[b], pb_[:, 0:128], ALU.add, [pbk, K('P')], [K('P')])
                        else:
                            self.tt('dve', TT[b], Pm[b], pb_[:, 0:128], ALU.add, [pbk, K('P')], [K('TT')])
                        Q, Qk, Qn, Qnk = Qn, Qnk, Q, Qk
                        Qt, Qtk, Qtn, Qtnk = Qtn, Qtnk, Qt, Qtk
                    if dn_mode < 4:
                        continue
                    p5, p5k = psq.next()
                    self.mm(p5[:, 0:128], TT[b], vb, True, True, [K('TT'), K('vb')], [p5k])
                    self.mm(p5[:, 128:256], kbg, TT[b], True, True, [K('TT'), K('kbg')], [p5k])
                    self.cp('act', um[b], p5[:, 0:128], [p5k], [K('u')])
                    self.cp('dve', wT[b], p5[:, 128:256], [p5k], [K('wT')])
                    if dn_mode < 5:
                        continue
                    hs = [2 * hp + h2 for h2 in range(2)]
                    p6, p6k = psq.next()
                    for h2 in range(2):
                        h = hs[h2]
                        self.mm(p6[h2 * 64:(h2 + 1) * 64, 0:128], wT[b][:, h2 * 64:(h2 + 1) * 64], S16[d][h], True, True,
                                [K('wT'), ('DS16', d, h)], [p6k])
                    self.tt('dve', vn[b], um[b], p6[:, 0:128], ALU.subtract, [K('u'), p6k], [K('vn')])
                    if dn_mode < 6:
                        continue
                    p7, p7k = psq.next()
                    for h2 in range(2):
                        h = hs[h2]
                        self.mm(p7[h2 * 64:(h2 + 1) * 64, 0:128], qdT[:, h2 * 64:(h2 + 1) * 64], S16[d][h], True, False,
                                [K('fm'), ('DS16', d, h)], [p7k])
                    self.mm(p7[:, 0:128], Aqk[b], vn[b], False, True, [K('Aqk'), K('vn')], [p7k])
                    self.cp('act', ot[b], p7[:, 0:128], [p7k], [K('o')])
                    for h2 in range(2):
                        h = hs[h2]
                        P.dma('pool', self.dnO[s][d][t0:t0 + 64, h * 128:(h + 1) * 128], ot[b][h2 * 64:(h2 + 1) * 64, :], reads=[K('o')])
                    if dn_mode < 7:
                        continue
                    for h2 in range(2):
                        h = hs[h2]
                        p8, p8k = psq.next()
                        self.mm(p8[:, 0:128], kd[h2 * 64:(h2 + 1) * 64, :], vn[b][h2 * 64:(h2 + 1) * 64, :], True, True,
                                [K('kd'), K('vn')], [p8k])
                        self.stt(Sst[d][h], Sst[d][h], egl[b][:, h2 * 2 + hp:h2 * 2 + hp + 1], p8[:, 0:128],
                                 ALU.mult, ALU.add, [('DS', d, h), K('egl'), p8k], [('DS', d, h)])
                        self.cp('act', S16[d][h], Sst[d][h], [('DS', d, h)], [('DS16', d, h)])
        P.barrier()
        self.arena_reset()
        of = Rot('Dof', [self.tile([128, 512], F32) for _ in range(2)])
        obw = Rot('Dobw', [self.tile([128, 512], F32) for _ in range(2)])
        gt_ = Rot('Dgt', [self.tile([128, 512], BF16) for _ in range(2)])
        gs = self.tile([128, 512], F32)
        sq = self.tile([128, 512], F32)
        ss = self.tile([128, 4], F32)
        on = self.tile([128, 512], BF16)
        stg = Rot('Dstg', [self.tile([128, 4, 512], BF16) for _ in range(2)])
        psr = self.psrot('Dps3', range(4))
        gn = self.PLv('dn_gn')
        for j in range(32 if sub('D3') else 0):
            t0 = j * 128
            a, ak = of.next()
            bb, bk = obw.next()
            g, gk = gt_.next()
            P.dma('sp', a, self.dnO[s][0][t0:t0 + 128, :], writes=[ak])
            P.dma('sp', bb, self.dnO[s][1][t0:t0 + 128, :], writes=[bk])
            P.dma('sp', g, self.zT[s][t0:t0 + 128, C_DNG:C_DNG + 512], writes=[gk])
            self.tt('dve', a, a, bb, ALU.add, [ak, bk], [ak])
            self.tt('pool', sq, a, a, ALU.mult, [ak], ['D3sq'])
            self.reduce_x(ss, sq.rearrange("p (h d) -> p h d", h=4), ['D3sq'], ['D3ss'])
            self.act(ss, ss, AF.Sqrt, ['D3ss'], ['D3ss'], bias=RMS_EPS, scale=1.0 / 128)
            self.recip(ss, ss, ['D3ss'], ['D3ss'])
            self.act(gs, g, AF.Silu, [gk], ['D3gs'])
            self.tt('pool', gs, gs, gn, ALU.mult, ['D3gs'], ['D3gs'])
            for h in range(4):
                self.stt(on[:, h * 128:(h + 1) * 128], a[:, h * 128:(h + 1) * 128], ss[:, h:h + 1], gs[:, h * 128:(h + 1) * 128],
                         ALU.mult, ALU.mult, [ak, 'D3ss', 'D3gs'], ['D3on'])
            if j % 4 == 0:
                st, stk = stg.next()
            pt, pk = psr.next()
            ptb = pt.bitcast(BF16)
            for h in range(4):
                self.tr(ptb[:, h * 128:(h + 1) * 128], on[:, h * 128:(h + 1) * 128], self.identb[:], ['D3on'], [pk])
            self.cp('act', st[:, :, (j % 4) * 128:(j % 4 + 1) * 128], ptb[:, 0:512].rearrange("p (h t) -> p h t", h=4), [pk], [stk])
            if j % 4 == 3:
                tt0 = (j // 4) * 512
                P.dma('pool', self.brT[s][3][:, tt0:tt0 + 512].rearrange("(h p) t -> p h t", p=128), st, reads=[stk])

    def phase_C(self, s, l, xsrc, xdst, last):
        P = self.P
        self.arena_reset()
        X32 = self.tile([128, 16, T], F32)
        H = self.tile([128, 16, T], BF16)
        ob_ = self.carve(32 * 1024)
        bigf = self.arena[:, ob_:ob_ + 8192]
        big = bigf.bitcast(BF16).rearrange("p (a b) -> p a b", a=32)
        Yv = bigf.rearrange("p (a b) -> p a b", a=16)
        BR = big[:, 0:16, :]
        MG = big[:, 16:32, :]
        wrot = Rot('Cw', [self.tile([128, 16, 512], BF16) for _ in range(2)])
        wbrot = Rot('Cwb', [self.tile([128, 4, 512], BF16) for _ in range(2)])
        sig = Rot('Csig', [self.tile([128, T], F32) for _ in range(2)])
        tmp = Rot('Ctmp', [self.tile([128, T], F32) for _ in range(2)])
        acc = [self.tile([128, T], F32) for _ in range(4)]
        rl = Rot('Crl', [self.tile([128, T], F32) for _ in range(2)])
        xv = xsrc[s].rearrange("(c p) t -> p c t", p=128)
        hv = self.hT[s].rearrange("(c p) t -> p c t", p=128)
        ov = xdst[s].rearrange("(c p) t -> p c t", p=128)
        brv = self.brT[s].rearrange("n (c p) t -> p (n c) t", p=128)
        wGv = self.wG.rearrange("(c p) n -> p c n", p=128)
        wBv = self.wB.rearrange("n (c p) d -> p n c d", p=128)
        wOv = self.wO.rearrange("(c p) n -> p c n", p=128)
        wUv = self.wU.rearrange("(c p) n -> p c n", p=128)
        wDv = self.wD.rearrange("(c p) n -> p c n", p=128)
        psG = self.psrot('CpsG', range(0, 3))
        psP = self.psrot('CpsP', range(3, 6))
        mark = self.aoff
        for i in range(NT):
            t0 = i * T
            self.aoff = mark
            P.dma('sp', X32, xv[:, :, t0:t0 + T], writes=['CX'])
            P.dma('sp', H, hv[:, :, t0:t0 + T], writes=['CH'])
            P.dma('sp', BR, brv[:, :, t0:t0 + T], writes=['CBR', 'CHID', 'CY'])
            for mg in range(4):
                for n in range(4):
                    wt, wk = wrot.next()
                    wb, wbk = wbrot.next()
                    P.dma('sp', wt, wGv[:, :, n * 2048 + mg * 512:n * 2048 + (mg + 1) * 512], writes=[wk])
                    P.dma('sp', wb, wBv[:, n, :, mg * 512:(mg + 1) * 512], writes=[wbk])
                    for mt in range(4):
                        pg, pgk = psG.next()
                        for c in range(16):
                            self.mm(pg, wt[:, c, mt * 128:(mt + 1) * 128], H[:, c, :], c == 0, c == 15, [wk, 'CH'], [pgk])
                        pp, ppk = psP.next()
                        for c in range(4):
                            self.mm(pp, wb[:, c, mt * 128:(mt + 1) * 128], BR[:, n * 4 + c, :], c == 0, c == 3, [wbk, 'CBR'], [ppk])
                        sg, sgk = sig.next()
                        self.act(sg, pg, AF.Sigmoid, [pgk], [sgk])
                        ak = ('Cacc', mt)
                        if n == 0:
                            self.tt('dve', acc[mt], sg, pp, ALU.mult, [sgk, ppk], [ak])
                        else:
                            tp, tpk = tmp.next()
                            self.tt('dve', tp, sg, pp, ALU.mult, [sgk, ppk], [tpk])
                            if n < 3:
                                self.tt('pool', acc[mt], acc[mt], tp, ALU.add, [tpk, ak], [ak])
                            else:
                                self.tt('pool', MG[:, mg * 4 + mt, :], acc[mt], tp, ALU.add, [tpk, ak], ['CMG'])
            for mg in range(4):
                wt, wk = wrot.next()
                P.dma('sp', wt, wOv[:, :, mg * 512:(mg + 1) * 512], writes=[wk])
                for mt in range(4):
                    pg, pgk = psG.next()
                    for c in range(16):
                        self.mm(pg, wt[:, c, mt * 128:(mt + 1) * 128], MG[:, c, :], c == 0, c == 15, [wk, 'CMG'], [pgk])
                    ch = mg * 4 + mt
                    self.tt('dve', X32[:, ch, :], X32[:, ch, :], pg, ALU.add, [pgk, 'CX'], ['CX'])
            self.rms_fm(lambda c: X32[:, c, :], 16, D, self.PLv('g_mlp'), H, 7, 'C', ['CX'], 'CH')
            for half in range(2):
                for ug in range(8):
                    ugg = half * 8 + ug
                    wt, wk = wrot.next()
                    P.dma('sp', wt, wUv[:, :, ugg * 512:(ugg + 1) * 512], writes=[wk])
                    for mt in range(4):
                        pg, pgk = psG.next()
                        for c in range(16):
                            self.mm(pg, wt[:, c, mt * 128:(mt + 1) * 128], H[:, c, :], c == 0, c == 15, [wk, 'CH'], [pgk])
                        r, rk = rl.next()
                        self.act(r, pg, AF.Relu, [pgk], [rk])
                        self.tt('pool', big[:, ug * 4 + mt, :], r, r, ALU.mult, [rk], ['CHID', 'CBR', 'CMG'])
                for ng in range(4):
                    pacc = [(self.ps[3 + mt][:], ('ps', 3 + mt)) for mt in range(4)]
                    for kg in range(2):
                        kgg = half * 2 + kg
                        wt, wk = wrot.next()
                        P.dma('sp', wt, wDv[:, kgg * 16:(kgg + 1) * 16, ng * 512:(ng + 1) * 512], writes=[wk])
                        for mt in range(4):
                            pt, pk = pacc[mt]
                            for c in range(16):
                                self.mm(pt, wt[:, c, mt * 128:(mt + 1) * 128], big[:, kg * 16 + c, :],
                                        kg == 0 and c == 0, kg == 1 and c == 15, [wk, 'CHID'], [pk])
                    for mt in range(4):
                        pt, pk = pacc[mt]
                        ch = ng * 4 + mt
                        self.tt('dve', X32[:, ch, :], X32[:, ch, :], pt, ALU.add, [pk, 'CX'], ['CX'])
            if not last:
                P.dma('pool', ov[:, :, t0:t0 + T], X32, reads=['CX'])
            else:
                Y = Yv
                sqr = Rot('Cfsq', [self.tile([128, T], F32) for _ in range(2)])
                sd = self.tile([128, T], F32)
                pss = self.ps[7][:]
                for c in range(16):
                    sq, k = sqr.next()
                    self.act(sq, X32[:, c, :], AF.Square, ['CX'], [k])
                    self.mm(pss, self.C('ones'), sq, c == 0, c == 15, [k], [('ps', 7)])
                self.act(sd, pss, AF.Sqrt, [('ps', 7)], ['Cfsd'], bias=RMS_EPS, scale=1.0 / D)
                self.recip(sd, sd, ['Cfsd'], ['Cfsd'])
                for c in range(16):
                    self.stt(Y[:, c, :], X32[:, c, :], self.gfin[:, c:c + 1], sd, ALU.mult, ALU.mult, ['CX', 'Cfsd'], ['CY', 'CHID'])
                o = P.dma('pool', ov[:, :, t0:t0 + T], Y, reads=['CY'])
                self.finals.append(o)


_NC_CACHE = {}


def host_inputs_common(inp):
    hc = host_constants()
    drow, dcol = hc['na_idx']
    rpb = np.asarray(inp['na_rpb'], np.float32)
    g = rpb[:, :, drow, dcol]
    rpbT = np.ascontiguousarray(g.transpose(0, 1, 3, 2, 4).reshape(DEPTH, 8, 128, 21 * 128))
    pl = np.stack([pack_layer_params(inp, l) for l in range(DEPTH)], 0)
    common = {
        "w_in": np.ascontiguousarray(inp['w_in'], dtype=np.float32),
        "mla_w_uq": np.ascontiguousarray(inp['mla_w_uq'], dtype=np.float32),
        "mla_w_ukv": np.ascontiguousarray(inp['mla_w_ukv'], dtype=np.float32),
        "w_branch": np.ascontiguousarray(inp['w_branch'], dtype=np.float32),
        "w_out": np.ascontiguousarray(inp['w_out'], dtype=np.float32),
        "w_up": np.ascontiguousarray(inp['w_up'], dtype=np.float32),
        "w_down": np.ascontiguousarray(inp['w_down'], dtype=np.float32),
        "pl": pl,
        "gfin": np.ascontiguousarray(np.asarray(inp['norm_final'], np.float32).reshape(16, 128).T),
        "cp": hc['cp'],
        "rope": hc['rope'],
        "hyz": hc['hyz'],
        "dftf": hc['dftf'],
        "dftny": hc['dftny'],
        "dfti": hc['dfti'],
        "rpbT": rpbT,
        "namask": hc['namask'],
        "dn_conv": np.ascontiguousarray(inp['dn_conv'], dtype=np.float32),
    }
    return common


def kernel(**inputs):
    inp = {k: np.asarray(v) for k, v in inputs.items()}
    xp = inp['x_prompt'].astype(np.float32, copy=False)
    xs = inp['x_sample'].astype(np.float32, copy=False)
    ncores = 8
    common = host_inputs_common(inp)
    in_maps = []
    for c in range(ncores):
        second = xs[c] if c < xs.shape[0] else xs[c % xs.shape[0]]
        xT = np.ascontiguousarray(np.stack([xp[c].T, second.T], 0))
        m = dict(common)
        m["xT"] = xT
        in_maps.append(m)
    if 'nc' not in _NC_CACHE:
        _NC_CACHE['nc'] = Builder(nseq=2).build()
    nc = _NC_CACHE['nc']
    res = run_bass_kernel_spmd(nc, in_maps, core_ids=list(range(ncores)))
    outs = [r["yT"] for r in res.results]
    y_prompt = np.ascontiguousarray(np.stack([outs[c][0].T for c in range(8)], 0))
    y_sample = np.ascontiguousarray(np.stack([outs[c][1].T for c in range(2)], 0))
    return (y_prompt, y_sample)
```

```python
import contextlib
import math
import numpy as np
import ml_dtypes
import concourse.bass as bass
import concourse.mybir as mybir
from concourse.bass_utils import run_bass_kernel_spmd

F32 = mybir.dt.float32
BF16 = mybir.dt.bfloat16
AF = mybir.ActivationFunctionType
ALU = mybir.AluOpType
AX = mybir.AxisListType

L = 4096
D = 2048
DEPTH = 2
T = 512
NT = L // T
RMS_EPS = 1e-6
IN_COLS = 14160
D_FF = 8192
NFM = 3456
NTM = 2560
R_NAQ, R_NAK, R_HY, R_CQ, R_CKV, R_KR, R_KRS = 0, 512, 1024, 2560, 3072, 3328, 3392
C_NAV, C_DNQ, C_DNK, C_DNV, C_DNG = 0, 512, 1024, 1536, 2048

ENGS = ['pe', 'act', 'dve', 'pool', 'sp']
DMA_POOL = 8
EPOCH = 20000


class Prog:
    def __init__(self, nc):
        self.nc = nc
        self.ops = {e: [] for e in ENGS}
        self.res_w = {}
        self.res_r = {}
        self.pending_barrier = {e: set() for e in ENGS}

    def op(self, eng, fn, reads=(), writes=(), dma=False, acc=False):
        idx = len(self.ops[eng])
        deps = set()
        for r in reads:
            w = self.res_w.get(r)
            if w is not None:
                deps.add(w)
            if isinstance(r, tuple) and r[0] == 'ps':
                for rd in self.res_r.get(r, ()):
                    if rd[0] != eng:
                        deps.add(rd)
        for r in writes:
            w = self.res_w.get(r)
            if w is not None and not (acc and w[0] == eng):
                deps.add(w)
            for rd in self.res_r.get(r, ()):
                deps.add(rd)
        if self.pending_barrier[eng]:
            deps |= self.pending_barrier[eng]
            self.pending_barrier[eng] = set()
        deps.discard((eng, idx))
        if eng == 'pe':
            deps = {d for d in deps if d[0] != 'pe'}
        self.ops[eng].append(dict(fn=fn, deps=deps, dma=dma, sig=False))
        for r in reads:
            self.res_r.setdefault(r, []).append((eng, idx))
        for r in writes:
            self.res_w[r] = (eng, idx)
            self.res_r[r] = []
        return (eng, idx)

    def dma(self, q, out, in_, reads=(), writes=()):
        return self.op(q, lambda e: e.dma_start(out=out, in_=in_), reads, writes, dma=True)

    def barrier(self):
        deps = set()
        for e in ENGS:
            n = len(self.ops[e])
            if n == 0:
                continue
            if e in ('sp', 'pool'):
                k = 0
                i = n - 1
                while i >= 0 and k < DMA_POOL:
                    if self.ops[e][i]['dma']:
                        k += 1
                    deps.add((e, i))
                    i -= 1
            else:
                deps.add((e, n - 1))
        for e in ENGS:
            self.pending_barrier[e] |= deps
        self.res_w = {}
        self.res_r = {}

    def emit(self, final_waits=()):
        nc = self.nc
        ops = self.ops
        for e in ENGS:
            for o in ops[e]:
                for (e2, i2) in o['deps']:
                    ops[e2][i2]['sig'] = True
        for (e2, i2) in final_waits:
            ops[e2][i2]['sig'] = True
        sem_keys = set()
        for e in ENGS:
            cnt = 0
            dcnt = 0
            for o in ops[e]:
                if o['dma']:
                    k = dcnt
                    dcnt += 1
                    key = ('d', e, k % DMA_POOL)
                    o['signal'] = (key, 16 * (k // DMA_POOL + 1))
                    o['prewait'] = (key, 16 * (k // DMA_POOL)) if k >= DMA_POOL else None
                    sem_keys.add(key)
                elif o['sig']:
                    key = ('c', e, cnt // EPOCH)
                    o['signal'] = (key, cnt % EPOCH + 1)
                    cnt += 1
                    sem_keys.add(key)
                else:
                    o['signal'] = None
        with contextlib.ExitStack() as es:
            sems = {}
            for key in sorted(sem_keys):
                sems[key] = es.enter_context(nc.semaphore("s_%s_%s_%d" % key))
            block = es.enter_context(nc.Block())

            def make(ename):
                def body(eng):
                    waited = {}
                    for o in ops[ename]:
                        need = {}
                        if o['dma'] and o['prewait'] is not None:
                            k, v = o['prewait']
                            need[k] = v
                        for (e2, i2) in o['deps']:
                            k, v = ops[e2][i2]['signal']
                            if v > need.get(k, 0):
                                need[k] = v
                        for k, v in need.items():
                            if waited.get(k, 0) >= v:
                                continue
                            eng.wait_ge(sems[k], v)
                            waited[k] = v
                        ins = o['fn'](eng)
                        if o['signal'] is not None:
                            k, v = o['signal']
                            ins.then_inc(sems[k], 16 if o['dma'] else 1)
                    if ename == 'sp':
                        for (e2, i2) in final_waits:
                            k, v = ops[e2][i2]['signal']
                            if waited.get(k, 0) < v:
                                eng.wait_ge(sems[k], v)
                                waited[k] = v
                return body

            block.tensor(make('pe'))
            block.scalar(make('act'))
            block.vector(make('dve'))
            block.gpsimd(make('pool'))
            block.sync(make('sp'))


class Rot:
    def __init__(self, name, tiles, keys=None):
        self.name = name
        self.tiles = tiles
        self.keys = keys
        self.i = 0

    def next(self):
        j = self.i % len(self.tiles)
        self.i += 1
        return self.tiles[j], (self.keys[j] if self.keys else (self.name, j))


_CONST = {}


def _bf16(a):
    return np.ascontiguousarray(a.astype(ml_dtypes.bfloat16))


def na_tables():
    cases = []
    for j in range(5):
        cases.append((2, j))
    for m in (0, 1):
        for kt in range(4):
            cases.append((m, kt))
    for m in (30, 31):
        for kt in range(28, 32):
            cases.append((m, kt))
    mask = np.zeros((21, 128, 128), np.float32)
    drow = np.zeros((21, 128, 128), np.int64)
    dcol = np.zeros((21, 128, 128), np.int64)
    a = np.arange(128) // 64
    c = np.arange(128) % 64
    for ci, (m, kt) in enumerate(cases):
        kr = (2 * kt + a)[:, None]
        kc = c[:, None]
        qr = (2 * m + a)[None, :]
        qc = c[None, :]
        rs = np.clip(qr - 4, 0, 56)
        cs = np.clip(qc - 8, 0, 48)
        ok = (kr >= rs) & (kr < rs + 8) & (kc >= cs) & (kc < cs + 16)
        mask[ci] = ok
        drow[ci] = np.clip(kr - qr + 7, 0, 14)
        dcol[ci] = np.clip(kc - qc + 15, 0, 30)
    return mask, drow, dcol


def na_case(m, kt):
    if 2 <= m <= 29:
        return kt - (m - 2)
    if m == 0:
        return 5 + kt
    if m == 1:
        return 9 + kt
    if m == 30:
        return 13 + (kt - 28)
    return 17 + (kt - 28)


def na_kts(m):
    if 2 <= m <= 29:
        return list(range(m - 2, m + 3))
    if m < 2:
        return [0, 1, 2, 3]
    return [28, 29, 30, 31]


CP = {}


def _cp_layout():
    off = 0
    for name, n in [('ident', 128), ('ones', 128), ('maskB0', 128), ('maskBi0', 128), ('maskB1', 128), ('maskBi1', 128),
                    ('tri0', 128), ('tri1', 128), ('onesblk', 128), ('sel0', 128), ('sel1', 128),
                    ('negt', 32), ('deltas', 512)]:
        CP[name] = (off, n)
        off += n
    return off


NCP = _cp_layout()


def host_constants():
    if _CONST:
        return _CONST
    f32 = np.float32
    cp = np.zeros((128, NCP), f32)

    def put(name, arr):
        o, n = CP[name]
        cp[:arr.shape[0], o:o + n] = arr
    put('ident', np.eye(128, dtype=f32))
    put('ones', np.ones((128, 128), f32))
    blk = (np.arange(128)[:, None] // 64) == (np.arange(128)[None, :] // 64)
    s_ = (np.arange(128) % 64)[:, None]
    c_ = (np.arange(128) % 64)[None, :]
    put('maskB0', (blk & (s_ < c_)).astype(f32))
    put('maskBi0', (blk & (s_ <= c_)).astype(f32))
    put('maskB1', (blk & (s_ > c_)).astype(f32))
    put('maskBi1', (blk & (s_ >= c_)).astype(f32))
    put('tri0', (blk & (s_ <= c_)).astype(f32))
    put('tri1', (blk & (s_ >= c_)).astype(f32))
    put('onesblk', blk.astype(f32))
    sel0 = np.zeros((128, 128), f32)
    sel0[:64] = 1
    sel1 = np.zeros((128, 128), f32)
    sel1[64:] = 1
    put('sel0', sel0)
    put('sel1', sel1)
    t = np.linspace(0.0, 1.0, L, dtype=f32)
    put('negt', (-t).reshape(32, 128).T)
    max_decay = math.log(1e-2) / 0.3
    min_decay = math.log(1e-2) / 1.5
    deltas = np.abs(np.linspace(min_decay, max_decay, 512, dtype=f32))
    put('deltas', np.tile(deltas[None, :], (128, 1)))
    mask, drow, dcol = na_tables()
    _CONST['namask'] = np.ascontiguousarray(mask.transpose(1, 0, 2).reshape(128, 21 * 128))
    _CONST['cp'] = cp
    _CONST['na_idx'] = (drow, dcol)
    inv = (10000.0 ** (-np.arange(32, dtype=f32) / 32)).astype(f32)
    ang = (np.arange(L, dtype=f32)[:, None] * inv[None, :]).astype(f32)
    cos = np.cos(ang).T.astype(f32)
    sin = np.sin(ang).T.astype(f32)
    _CONST['rope'] = np.ascontiguousarray(np.stack([np.concatenate([cos, cos], 0), np.concatenate([-sin, sin], 0)], 0))
    w = (2.0 * np.pi * np.arange(L, dtype=f32) / L).astype(f32)[:, None]
    bands = np.linspace(1e-4, 15, 16, dtype=f32)[None, :]
    z = np.concatenate([t[:, None], np.cos(bands * w), -np.sin(bands * w)], axis=-1).astype(f32)
    _CONST['hyz'] = np.ascontiguousarray(z.T)
    n = 2 * L
    tt = np.arange(L, dtype=np.int64)
    R = np.arange(n, dtype=np.int64)
    f = np.where(R <= L, R, R - L)
    ph = (tt[:, None] * f[None, :]) % n
    angm = 2.0 * np.pi * ph.astype(np.float64) / n
    fwd = np.where((R <= L)[None, :], np.cos(angm), np.sin(angm))
    fwd[:, L] = np.cos(np.pi * tt)
    scale = np.full(n, 2.0 / n)
    scale[0] = 1.0 / n
    scale[L] = 1.0 / n
    inv_m = (fwd * scale[None, :]).T
    fw = fwd.reshape(32, 128, 64, 128).transpose(2, 1, 0, 3)
    _CONST['dftf'] = _bf16(fw)
    _CONST['dftny'] = _bf16(fwd[:, L].reshape(32, 128).T)
    iv = inv_m.reshape(4, 16, 128, 8, 512).transpose(3, 0, 2, 1, 4)
    _CONST['dfti'] = _bf16(iv)
    return _CONST


PL = {}


def _pl_layout():
    off = 0
    for name, n in [('g_mix', 16), ('g_mlp', 16), ('g_q', 4), ('g_kv', 2), ('hy_short', 36), ('hy_skip', 4),
                    ('hy_w1', 64), ('hy_b1', 1), ('hy_w2', 64), ('hy_b2', 1), ('hy_w3', 1024),
                    ('dn_alog', 8), ('dn_dtb', 8), ('dn_gn', 512)]:
        PL[name] = (off, n)
        off += n
    return off


NPL = _pl_layout()


def pack_layer_params(inp, l):
    f32 = np.float32
    pl = np.zeros((128, NPL), f32)

    def put(name, arr):
        o, n = PL[name]
        pl[:arr.shape[0], o:o + n] = arr
    put('g_mix', inp['norm_mix'][l].reshape(16, 128).T)
    put('g_mlp', inp['norm_mlp'][l].reshape(16, 128).T)
    put('g_q', inp['mla_g_q'][l].reshape(4, 128).T)
    put('g_kv', inp['mla_g_kv'][l].reshape(2, 128).T)
    put('hy_short', inp['hy_short'][l].reshape(3, 12, 128).transpose(2, 0, 1).reshape(128, 36))
    put('hy_skip', inp['hy_skip'][l].reshape(4, 128).T)
    put('hy_w1', inp['hy_w1'][l])
    put('hy_b1', inp['hy_b1'][l][:, None])
    put('hy_w2', inp['hy_w2'][l])
    put('hy_b2', inp['hy_b2'][l][:, None])
    put('hy_w3', inp['hy_w3'][l])
    put('dn_alog', np.tile(inp['dn_a_log'][l].reshape(1, 8), (128, 1)))
    put('dn_dtb', np.tile(inp['dn_dt_bias'][l].reshape(1, 8), (128, 1)))
    put('dn_gn', np.tile(inp['dn_g_norm'][l].reshape(1, 128), (128, 4)))
    return pl


class Builder:
    def __init__(self, nseq=2, depth=DEPTH, debug=False, phases=None):
        self.nseq = nseq
        self.depth = depth
        self.debug = debug
        self.phases = phases
        self.nc = bass.Bass("TRN2", target_bir_lowering=False)
        self.P = Prog(self.nc)
        self.es = contextlib.ExitStack()
        self.finals = []

    def din(self, name, shape, dt=F32):
        return self.nc.dram_tensor(name, list(shape), dt, kind="ExternalInput").ap()

    def dout(self, name, shape, dt=F32):
        return self.nc.dram_tensor(name, list(shape), dt, kind="ExternalOutput").ap()

    def dscr(self, name, shape, dt, dbg=False):
        kind = "ExternalOutput" if (dbg and self.debug) else "Internal"
        return self.nc.dram_tensor(name, list(shape), dt, kind=kind).ap()

    def carve(self, nbytes_per_part):
        n = (nbytes_per_part + 3) // 4
        o = self.aoff
        self.aoff += n
        assert self.aoff <= self.asize, ("arena overflow", self.aoff * 4)
        return o

    def tile(self, shape, dt):
        n = int(np.prod(shape[1:]))
        esz = 2 if dt == BF16 else 4
        o = self.carve(n * esz)
        v = self.arena[:, o:o + (n * esz + 3) // 4]
        if dt != F32:
            v = v.bitcast(dt)
            v = v[:, 0:n]
        if len(shape) == 3:
            v = v.rearrange("p (a b) -> p a b", a=shape[1])
        elif len(shape) == 4:
            v = v.rearrange("p (a b c) -> p a b c", a=shape[1], b=shape[2])
        return v

    def arena_reset(self):
        self.aoff = self.abase

    def mm(self, out, lhsT, rhs, start, stop, reads, writes):
        self.P.op('pe', lambda e: e.matmul(out, lhsT=lhsT, rhs=rhs, start=start, stop=stop), reads, writes, acc=not start)

    def tr(self, out, in_, ident, reads, writes):
        self.P.op('pe', lambda e: e.transpose(out, in_, ident), reads, writes)

    def act(self, out, in_, func, reads, writes, bias=None, scale=None):
        kw = {}
        if bias is not None:
            kw['bias'] = bias
        if scale is not None:
            kw['scale'] = scale
        self.P.op('act', lambda e: e.activation(out=out, in_=in_, func=func, **kw), reads, writes)

    def tt(self, eng, out, in0, in1, op, reads, writes):
        self.P.op(eng, lambda e: e.tensor_tensor(out=out, in0=in0, in1=in1, op=op), reads, writes)

    def ts(self, eng, out, in0, s1, op0, reads, writes, s2=None, op1=None):
        if op1 is None:
            self.P.op(eng, lambda e: e.tensor_scalar(out=out, in0=in0, scalar1=s1, scalar2=None, op0=op0), reads, writes)
        else:
            self.P.op(eng, lambda e: e.tensor_scalar(out=out, in0=in0, scalar1=s1, scalar2=s2, op0=op0, op1=op1), reads, writes)

    def stt(self, out, in0, scalar, in1, op0, op1, reads, writes):
        self.P.op('dve', lambda e: e.scalar_tensor_tensor(out=out, in0=in0, scalar=scalar, in1=in1, op0=op0, op1=op1), reads, writes)

    def cp(self, eng, out, in_, reads, writes):
        if eng == 'act':
            self.P.op('act', lambda e: e.copy(out=out, in_=in_), reads, writes)
        else:
            self.P.op(eng, lambda e: e.tensor_copy(out=out, in_=in_), reads, writes)

    def reduce_x(self, out, in_, reads, writes):
        self.P.op('dve', lambda e: e.tensor_reduce(out=out, in_=in_, axis=AX.X, op=ALU.add), reads, writes)

    def recip(self, out, in_, reads, writes):
        self.P.op('dve', lambda e: e.reciprocal(out=out, in_=in_), reads, writes)

    def psrot(self, name, idxs, width=None):
        idxs = list(idxs)
        tiles = [self.ps[i][:] if width is None else self.ps[i][:, 0:width] for i in idxs]
        return Rot(name, tiles, [('ps', i) for i in idxs])

    def C(self, name):
        o, n = CP[name]
        return self.cpt[:, o:o + n]

    def PLv(self, name):
        o, n = PL[name]
        return self.plt[:, o:o + n]

    def build(self):
        nc, P, es = self.nc, self.P, self.es
        S = self.nseq
        dbg = self.debug
        self.xT_in = self.din("xT", [S, D, L])
        self.w_in = self.din("w_in", [DEPTH, D, IN_COLS])
        self.w_uq = self.din("mla_w_uq", [DEPTH, 512, 768])
        self.w_ukv = self.din("mla_w_ukv", [DEPTH, 256, 1024])
        self.w_branch = self.din("w_branch", [DEPTH, 4, 512, D])
        self.w_out = self.din("w_out", [DEPTH, D, D])
        self.w_up = self.din("w_up", [DEPTH, D, D_FF])
        self.w_down = self.din("w_down", [DEPTH, D_FF, D])
        self.pl_in = self.din("pl", [DEPTH, 128, NPL])
        self.gfin_in = self.din("gfin", [128, 16])
        self.cp_in = self.din("cp", [128, NCP])
        self.rope_in = self.din("rope", [2, 64, L])
        self.hyz_in = self.din("hyz", [33, L])
        self.dftf_in = self.din("dftf", [64, 128, 32, 128], BF16)
        self.dftny_in = self.din("dftny", [128, 32], BF16)
        self.dfti_in = self.din("dfti", [8, 4, 128, 16, 512], BF16)
        self.rpbT_in = self.din("rpbT", [DEPTH, 8, 128, 21 * 128])
        self.namask_in = self.din("namask", [128, 21 * 128])
        self.dnconv_in = self.din("dn_conv", [DEPTH, 3, 1536])
        self.yT = self.dout("yT", [S, D, L])
        self.xT_mid = self.dscr("xT_mid", [S, D, L], F32, dbg)
        self.hT = self.dscr("hT", [S, D, L], BF16, dbg)
        self.zF = self.dscr("zF", [S, NFM, L], BF16, dbg)
        self.zT = self.dscr("zT", [S, L, NTM], BF16, dbg)
        self.zAB = self.dscr("zAB", [S, L, 16], F32, dbg)
        self.brT = self.dscr("brT", [S, 4, 512, L], BF16, dbg)
        self.wA_l = [self.dscr("wA%d" % i, [D, 6144], BF16) for i in range(2)]
        self.wAB_l = [self.dscr("wAB%d" % i, [D, 16], BF16) for i in range(2)]
        self.wG_l = [self.dscr("wG%d" % i, [D, 8192], BF16) for i in range(2)]
        self.wB_l = [self.dscr("wB%d" % i, [4, 512, D], BF16) for i in range(2)]
        self.wO_l = [self.dscr("wO%d" % i, [D, D], BF16) for i in range(2)]
        self.wU_l = [self.dscr("wU%d" % i, [D, D_FF], BF16) for i in range(2)]
        self.wD_l = [self.dscr("wD%d" % i, [D_FF, D], BF16) for i in range(2)]
        self.wUQ_l = [self.dscr("wUQ%d" % i, [512, 4, 256], BF16) for i in range(2)]
        self.wUKV_l = [self.dscr("wUKV%d" % i, [256, 1024], BF16) for i in range(2)]
        self.cast_queue = []
        self.mqn = self.dscr("mqn", [S, 4, 128, L], BF16, dbg)
        self.mqr = self.dscr("mqr", [S, 4, 64, L], BF16, dbg)
        self.mkn = self.dscr("mkn", [S, 4, 128, L], BF16, dbg)
        self.mkr = self.dscr("mkr", [S, 64, L], BF16, dbg)
        self.mv = self.dscr("mv", [S, L, 512], BF16, dbg)
        self.naE = self.dscr("naE", [8, 128, 21 * 128], BF16, dbg)
        self.hyK = self.dscr("hyK", [2, L, 512], BF16, dbg)
        self.hyKf = self.dscr("hyKf", [64, 128, 512], F32, dbg)
        self.hyX2 = self.dscr("hyX2", [S, 512, L], F32, dbg)
        self.hyVX = self.dscr("hyVX", [S, 512, L], F32, dbg)
        self.dnQ = self.dscr("dnQ", [S, L, 512], BF16, dbg)
        self.dnK = self.dscr("dnK", [S, L, 512], BF16, dbg)
        self.dnV = self.dscr("dnV", [S, L, 512], BF16, dbg)
        self.dnGB = self.dscr("dnGB", [S, L, 16], F32, dbg)
        self.dnO = self.dscr("dnO", [S, 2, L, 512], F32, dbg)

        sb = lambda n, s, d: es.enter_context(nc.sbuf_tensor(n, s, d))
        self.cpt = sb("cpt", [128, NCP], F32)
        self.plt = sb("plt", [128, NPL], F32)
        self.gfin = sb("gfint", [128, 16], F32)
        self.identb = sb("identb", [128, 128], BF16)
        self.onesb = sb("onesb", [128, 128], BF16)
        self.asize = (176 * 1024) // 4
        self.arena = sb("arena", [128, self.asize], F32)
        self.abase = 0
        self.aoff = 0
        self.ps = [es.enter_context(nc.psum_tensor("ps%d" % i, [128, 512], F32)) for i in range(8)]

        P.dma('sp', self.cpt[:], self.cp_in, writes=['cpt'])
        P.dma('sp', self.gfin[:], self.gfin_in, writes=['gfin'])
        self.cp('dve', self.identb[:], self.C('ident'), ['cpt'], ['identb'])
        self.cp('dve', self.onesb[:], self.C('ones'), ['cpt'], ['onesb'])
        P.barrier()

        run = (lambda ph: self.phases is None or ph in self.phases)
        for l in range(self.depth):
            xsrc = self.xT_in if l == 0 else self.xT_mid
            last = (l == self.depth - 1)
            xdst = self.yT if last else self.xT_mid
            P.dma('sp', self.plt[:], self.pl_in[l], writes=['plt'])
            P.barrier()
            self.set_layer_weights(l)
            if run('W') and l == 0:
                casts = self.weight_casts(0)
                for (dst, src) in casts[:16]:
                    P.dma('pool', dst, src)
                self.cast_queue = casts[16:]
                P.barrier()
            if run('F'):
                self.hy_filters(l)
                P.barrier()
                self.na_tables_dev(l)
                P.barrier()
            for s in range(S):
                if run('A'):
                    self.phase_A(s, l, xsrc, 8 if (l == 0 and s == 0) else 2)
                    self.pump_casts(10 ** 6 if (l == 0 and s == 0) else 0)
                    P.barrier()
                    if run('W') and l == 0 and s == 0 and self.depth > 1:
                        self.cast_queue = self.weight_casts(1)
                if run('M'):
                    self.mla(s)
                    P.barrier()
                if run('N'):
                    self.na(s)
                    P.barrier()
                if run('H'):
                    self.hyena(s)
                    P.barrier()
                if run('D') or run('D1') or run('D2') or run('D3'):
                    self.deltanet(s, l)
                    P.barrier()
                if run('C'):
                    self.phase_C(s, l, xsrc, xdst, last)
                    P.barrier()
            self.pump_casts(10 ** 6)
            P.barrier()
        P.emit(final_waits=self.finals)
        return nc

    def set_layer_weights(self, l):
        i = l % 2
        self.wA, self.wAB, self.wG, self.wB, self.wO = self.wA_l[i], self.wAB_l[i], self.wG_l[i], self.wB_l[i], self.wO_l[i]
        self.wU, self.wD, self.wUQ, self.wUKV = self.wU_l[i], self.wD_l[i], self.wUQ_l[i], self.wUKV_l[i]

    def pump_casts(self, n):
        while n > 0 and self.cast_queue:
            dst, src = self.cast_queue.pop(0)
            self.P.dma('pool', dst, src)
            n -= 1

    def weight_casts(self, l):
        saved = (self.wA, self.wAB, self.wG, self.wB, self.wO, self.wU, self.wD, self.wUQ, self.wUKV) if hasattr(self, 'wA') else None
        self.set_layer_weights(l)
        out = []
        w = self.w_in[l]

        def cast(dst, src):
            out.append((dst, src))
        cast(self.wA[:, 0:512], w[:, 0:512])
        cast(self.wA[:, 512:1024], w[:, 512:1024])
        for j in range(3):
            cast(self.wA[:, 1024 + j * 512:1536 + j * 512], w[:, 1536 + j * 512:2048 + j * 512])
        cast(self.wA[:, 2560:3072], w[:, 3072:3584])
        cast(self.wA[:, 3072:3328], w[:, 3584:3840])
        cast(self.wA[:, 3328:3392], w[:, 3840:3904])
        cast(self.wA[:, 3392:3424], w[:, 3872:3904])
        cast(self.wA[:, 3424:3456], w[:, 3840:3872])
        cast(self.wA[:, 3584:4096], w[:, 1024:1536])
        for j in range(3):
            cast(self.wA[:, 4096 + j * 512:4608 + j * 512], w[:, 3904 + j * 512:4416 + j * 512])
        cast(self.wA[:, 5632:6144], w[:, 5440:5952])
        cast(self.wAB[:, :], w[:, 5952:5968])
        assert len(out) == 16
        uq = self.w_uq[l].rearrange("k (h c) -> k h c", h=4)
        cast(self.wUQ[:, :, 0:192], uq)
        cast(self.wUQ[:, :, 192:224], uq[:, :, 160:192])
        cast(self.wUQ[:, :, 224:256], uq[:, :, 128:160])
        ukv = self.w_ukv[l].rearrange("k (h c) -> k h c", h=4)
        cast(self.wUKV[:, 0:512].rearrange("k (h c) -> k h c", h=4), ukv[:, :, 0:128])
        cast(self.wUKV[:, 512:1024].rearrange("k (h c) -> k h c", h=4), ukv[:, :, 128:256])
        for j in range(16):
            cast(self.wG[:, j * 512:(j + 1) * 512], w[:, 5968 + j * 512:5968 + (j + 1) * 512])
        wb = self.w_branch[l].rearrange("n k d -> (n k) d")
        wbd = self.wB.rearrange("n k d -> (n k) d")
        for j in range(4):
            cast(wbd[:, j * 512:(j + 1) * 512], wb[:, j * 512:(j + 1) * 512])
            cast(self.wO[:, j * 512:(j + 1) * 512], self.w_out[l][:, j * 512:(j + 1) * 512])
        for j in range(16):
            cast(self.wU[:, j * 512:(j + 1) * 512], self.w_up[l][:, j * 512:(j + 1) * 512])
        for r in range(4):
            for j in range(4):
                cast(self.wD[r * 2048:(r + 1) * 2048, j * 512:(j + 1) * 512],
                     self.w_down[l][r * 2048:(r + 1) * 2048, j * 512:(j + 1) * 512])
        if saved is not None:
            (self.wA, self.wAB, self.wG, self.wB, self.wO, self.wU, self.wD, self.wUQ, self.wUKV) = saved
        return out

    def rms_fm(self, src_chunks, nch, nfeat, g_ap, dst, psi, tag, reads, dst_key):
        ps_ss = self.ps[psi][:]
        ssk = ('ps', psi)
        sqr = Rot(tag + 'sq', [self.tile([128, T], F32) for _ in range(2)])
        sd = self.tile([128, T], F32)
        rstd = self.tile([128, T], F32)
        for c in range(nch):
            sq, k = sqr.next()
            self.act(sq, src_chunks(c), AF.Square, reads, [k])
            self.mm(ps_ss, self.C('ones'), sq, c == 0, c == nch - 1, [k], [ssk])
        self.act(sd, ps_ss, AF.Sqrt, [ssk], [tag + 'sd'], bias=RMS_EPS, scale=1.0 / nfeat)
        self.recip(rstd, sd, [tag + 'sd'], [tag + 'rstd'])
        for c in range(nch):
            self.stt(dst[:, c, :], src_chunks(c), g_ap[:, c:c + 1], rstd, ALU.mult, ALU.mult,
                     list(reads) + [tag + 'rstd'], [dst_key])

    def phase_A(self, s, l, xsrc, npump=0):
        P = self.P
        self.arena_reset()
        X32 = self.tile([128, 16, T], F32)
        H = self.tile([128, 16, T], BF16)
        wrot = Rot('Aw', [self.tile([128, 16, 512], BF16) for _ in range(2)])
        wab = self.tile([128, 16, 16], BF16)
        strot = Rot('Ast', [self.tile([128, 512], BF16) for _ in range(4)])
        stab = self.tile([128, 4, 16], F32)
        psr = self.psrot('Aps', range(1, 7))
        xv = xsrc[s].rearrange("(c p) t -> p c t", p=128)
        hv = self.hT[s].rearrange("(c p) t -> p c t", p=128)
        P.dma('sp', wab, self.wAB.rearrange("(c p) n -> p c n", p=128), writes=['Awab'])
        for i in range(NT):
            t0 = i * T
            P.dma('sp', X32, xv[:, :, t0:t0 + T], writes=['AX'])
            self.rms_fm(lambda c: X32[:, c, :], 16, D, self.PLv('g_mix'), H, 0, 'A', ['AX'], 'AH')
            P.dma('pool', hv[:, :, t0:t0 + T], H, reads=['AH'])
            ev = 0
            for g in range(7):
                nm = 4 if g < 6 else 3
                wt, wk = wrot.next()
                P.dma('sp', wt[:, :, 0:nm * 128], self.wA.rearrange("(c p) n -> p c n", p=128)[:, :, g * 512:g * 512 + nm * 128], writes=[wk])
                for m in range(nm):
                    pt, pk = psr.next()
                    for c in range(16):
                        self.mm(pt, wt[:, c, m * 128:(m + 1) * 128], H[:, c, :], c == 0, c == 15, [wk, 'AH'], [pk])
                    st, sk = strot.next()
                    self.cp('act' if ev % 2 == 0 else 'dve', st, pt, [pk], [sk])
                    ev += 1
                    r0 = g * 512 + m * 128
                    P.dma('pool', self.zF[s][r0:r0 + 128, t0:t0 + T], st, reads=[sk])
            for g in range(5):
                wt, wk = wrot.next()
                P.dma('sp', wt, self.wA.rearrange("(c p) n -> p c n", p=128)[:, :, 3584 + g * 512:3584 + (g + 1) * 512], writes=[wk])
                for j in range(4):
                    pt, pk = psr.next()
                    for c in range(16):
                        self.mm(pt, H[:, c, j * 128:(j + 1) * 128], wt[:, c, :], c == 0, c == 15, [wk, 'AH'], [pk])
                    st, sk = strot.next()
                    self.cp('act' if ev % 2 == 0 else 'dve', st, pt, [pk], [sk])
                    ev += 1
                    P.dma('pool', self.zT[s][t0 + j * 128:t0 + (j + 1) * 128, g * 512:(g + 1) * 512], st, reads=[sk])
            for j in range(4):
                pt, pk = psr.next()
                for c in range(16):
                    self.mm(pt[:, 0:16], H[:, c, j * 128:(j + 1) * 128], wab[:, c, :], c == 0, c == 15, ['Awab', 'AH'], [pk])
                self.cp('dve', stab[:, j, :], pt[:, 0:16], [pk], ['Astab'])
            P.dma('pool', self.zAB[s][t0:t0 + T, :].rearrange("(j p) n -> p j n", p=128), stab, reads=['Astab'])
            self.pump_casts(npump)

    def mla(self, s):
        P = self.P
        self.arena_reset()
        wuq = self.tile([128, 4, 4, 256], BF16)
        wukv = self.tile([128, 2, 1024], BF16)
        CQ = self.tile([128, 4, T], BF16)
        CKV = self.tile([128, 2, T], BF16)
        KR = self.tile([128, T], BF16)
        KRS = self.tile([128, T], BF16)
        CQN = self.tile([128, 4, T], BF16)
        CKVN = self.tile([128, 2, T], BF16)
        ROPE = self.tile([128, 2, T], F32)
        strot = Rot('Mst', [self.tile([128, 512], BF16) for _ in range(4)])
        tmpr = Rot('Mtmp', [self.tile([128, T], F32) for _ in range(2)])
        tmp2r = Rot('Mtmp2', [self.tile([128, T], F32) for _ in range(2)])
        psr = self.psrot('Mps', range(2, 8))
        P.dma('sp', wuq, self.wUQ.rearrange("(c p) h n -> p c h n", p=128), writes=['Mwuq'])
        P.dma('sp', wukv, self.wUKV.rearrange("(c p) n -> p c n", p=128), writes=['Mwukv'])
        zf = self.zF[s]
        mark = self.aoff
        for i in range(NT):
            t0 = i * T
            self.aoff = mark
            P.dma('sp', CQ, zf[R_CQ:R_CQ + 512, t0:t0 + T].rearrange("(c p) t -> p c t", p=128), writes=['MCQ'])
            P.dma('sp', CKV, zf[R_CKV:R_CKV + 256, t0:t0 + T].rearrange("(c p) t -> p c t", p=128), writes=['MCKV'])
            P.dma('sp', KR[0:64, :], zf[R_KR:R_KR + 64, t0:t0 + T], writes=['MKR'])
            P.dma('sp', KRS[0:64, :], zf[R_KRS:R_KRS + 64, t0:t0 + T], writes=['MKRS'])
            P.dma('sp', ROPE[0:64, :, :], self.rope_in[:, :, t0:t0 + T].rearrange("a p t -> p a t"), writes=['MROPE'])
            self.rms_fm(lambda c: CQ[:, c, :], 4, 512, self.PLv('g_q'), CQN, 0, 'Mq', ['MCQ'], 'MCQN')
            self.rms_fm(lambda c: CKV[:, c, :], 2, 256, self.PLv('g_kv'), CKVN, 1, 'Mk', ['MCKV'], 'MCKVN')
            ev = 0
            for h in range(4):
                pt, pk = psr.next()
                for c in range(4):
                    self.mm(pt, wuq[:, c, h, 0:128], CQN[:, c, :], c == 0, c == 3, ['Mwuq', 'MCQN'], [pk])
                st, sk = strot.next()
                self.cp('act', st, pt, [pk], [sk])
                P.dma('pool', self.mqn[s][h][:, t0:t0 + T], st, reads=[sk])
                pr, prk = psr.next()
                for c in range(4):
                    self.mm(pr[0:64, :], wuq[:, c, h, 128:192], CQN[:, c, :], c == 0, c == 3, ['Mwuq', 'MCQN'], [prk])
                pq, pqk = psr.next()
                for c in range(4):
                    self.mm(pq[0:64, :], wuq[:, c, h, 192:256], CQN[:, c, :], c == 0, c == 3, ['Mwuq', 'MCQN'], [pqk])
                ta, tak = tmpr.next()
                tb, tbk = tmp2r.next()
                self.tt('dve', ta[0:64, :], pr[0:64, :], ROPE[0:64, 0, :], ALU.mult, [prk, 'MROPE'], [tak])
                self.tt('dve', tb[0:64, :], pq[0:64, :], ROPE[0:64, 1, :], ALU.mult, [pqk, 'MROPE'], [tbk])
                st, sk = strot.next()
                self.tt('pool', st[0:64, :], ta[0:64, :], tb[0:64, :], ALU.add, [tak, tbk], [sk])
                P.dma('pool', self.mqr[s][h][:, t0:t0 + T], st[0:64, :], reads=[sk])
                pt, pk = psr.next()
                for c in range(2):
                    self.mm(pt, wukv[:, c, h * 128:(h + 1) * 128], CKVN[:, c, :], c == 0, c == 1, ['Mwukv', 'MCKVN'], [pk])
                st, sk = strot.next()
                self.cp('act', st, pt, [pk], [sk])
                P.dma('pool', self.mkn[s][h][:, t0:t0 + T], st, reads=[sk])
            for j in range(4):
                pt, pk = psr.next()
                for c in range(2):
                    self.mm(pt, CKVN[:, c, j * 128:(j + 1) * 128], wukv[:, c, 512:1024], c == 0, c == 1, ['Mwukv', 'MCKVN'], [pk])
                st, sk = strot.next()
                self.cp('act' if j % 2 else 'dve', st, pt, [pk], [sk])
                P.dma('pool', self.mv[s][t0 + j * 128:t0 + (j + 1) * 128, :], st, reads=[sk])
            ta, tak = tmpr.next()
            tb, tbk = tmp2r.next()
            self.tt('dve', ta[0:64, :], KR[0:64, :], ROPE[0:64, 0, :], ALU.mult, ['MKR', 'MROPE'], [tak])
            self.tt('dve', tb[0:64, :], KRS[0:64, :], ROPE[0:64, 1, :], ALU.mult, ['MKRS', 'MROPE'], [tbk])
            st, sk = strot.next()
            self.tt('pool', st[0:64, :], ta[0:64, :], tb[0:64, :], ALU.add, [tak, tbk], [sk])
            P.dma('pool', self.mkr[s][:, t0:t0 + T], st[0:64, :], reads=[sk])
        P.barrier()
        self.arena_reset()
        KN = Rot('MKN', [self.tile([128, L], BF16) for _ in range(2)])
        KRP = self.tile([128, L], BF16)
        VH = Rot('MVH', [self.tile([128, 32, 128], BF16) for _ in range(2)])
        QN = Rot('MQN', [self.tile([128, T], BF16) for _ in range(2)])
        QR = Rot('MQR', [self.tile([128, T], BF16) for _ in range(2)])
        PT = Rot('MPT', [self.tile([128, T], BF16) for _ in range(3)])
        rcp = self.tile([128, T], F32)
        ost = Rot('MOST', [self.tile([128, T], BF16) for _ in range(2)])
        psS = self.psrot('MpsS', range(0, 3))
        psO = self.psrot('MpsO', range(3, 5))
        psU = self.psrot('MpsU', range(5, 7))
        scale = 192.0 ** -0.5
        self.P.op('pool', lambda e, t=KRP: e.memset(t[64:128, :], 0.0), [], ['MKRPz'])
        for qt in QR.tiles:
            self.P.op('pool', lambda e, qt=qt: e.memset(qt[64:128, :], 0.0), [], ['MQRz'])
        P.dma('sp', KRP[0:64, :], self.mkr[s], writes=['MKRP'])
        for h in range(4):
            kn, knk = KN.next()
            vh, vhk = VH.next()
            P.dma('sp', kn, self.mkn[s][h], writes=[knk])
            P.dma('sp', vh, self.mv[s][:, h * 128:(h + 1) * 128].rearrange("(k p) d -> p k d", p=128), writes=[vhk])
            for i in range(NT):
                t0 = i * T
                qn, qnk = QN.next()
                qr, qrk = QR.next()
                P.dma('sp', qn, self.mqn[s][h][:, t0:t0 + T], writes=[qnk])
                P.dma('sp', qr[0:64, :], self.mqr[s][h][:, t0:t0 + T], writes=[qrk])
                po, pok = psO.next()
                pu, puk = psU.next()

                def scores(kt):
                    pS, pSk = psS.next()
                    self.mm(pS, kn[:, kt * 128:(kt + 1) * 128], qn, True, False, [knk, qnk], [pSk])
                    self.mm(pS, KRP[:, kt * 128:(kt + 1) * 128], qr, False, True, ['MKRP', 'MKRPz', 'MQRz', qrk], [pSk])
                    pt, ptk = PT.next()
                    self.act(pt, pS, AF.Exp, [pSk], [ptk], scale=scale)
                    return pt, ptk
                cur = scores(0)
                for kt in range(32):
                    nxt = scores(kt + 1) if kt < 31 else None
                    pt, ptk = cur
                    self.mm(po, vh[:, kt, :], pt, kt == 0, kt == 31, [vhk, ptk], [pok])
                    self.mm(pu, self.onesb[:], pt, kt == 0, kt == 31, [ptk], [puk])
                    cur = nxt
                self.recip(rcp, pu, [puk], ['Mrcp'])
                o, ok_ = ost.next()
                self.tt('dve', o, po, rcp, ALU.mult, [pok, 'Mrcp'], [ok_])
                P.dma('pool', self.brT[s][2][h * 128:(h + 1) * 128, t0:t0 + T], o, reads=[ok_])

    def na_tables_dev(self, l):
        P = self.P
        self.arena_reset()
        rp = Rot('NTr', [self.tile([128, 21 * 128], F32) for _ in range(2)])
        eo = Rot('NTe', [self.tile([128, 21 * 128], BF16) for _ in range(2)])
        msk = self.tile([128, 21 * 128], F32)
        P.dma('sp', msk, self.namask_in, writes=['NTm'])
        for h in range(8):
            r, rk = rp.next()
            e, ek = eo.next()
            P.dma('sp', r, self.rpbT_in[l][h], writes=[rk])
            self.act(r, r, AF.Exp, [rk], [rk])
            self.tt('dve', e, r, msk, ALU.mult, [rk, 'NTm'], [ek])
            P.dma('pool', self.naE[h], e, reads=[ek])

    def na(self, s):
        P = self.P
        self.arena_reset()
        V = self.tile([128, 32, 512], BF16)
        QT = Rot('NQ', [self.tile([128, L], BF16) for _ in range(2)])
        KT = Rot('NK', [self.tile([128, L], BF16) for _ in range(2)])
        E = Rot('NE', [self.tile([128, 21, 128], BF16) for _ in range(2)])
        OUT = Rot('NO', [self.tile([128, L], BF16) for _ in range(2)])
        ES = Rot('NES', [self.tile([128, 128], F32) for _ in range(3)])
        PT = Rot('NPT', [self.tile([128, 128], BF16) for _ in range(3)])
        rcp = self.tile([128, 128], F32)
        psS = self.psrot('NpsS', range(0, 3), 128)
        psO = self.psrot('NpsO', range(3, 5), 128)
        psU = self.psrot('NpsU', range(5, 7), 128)
        P.dma('sp', V, self.zT[s][:, C_NAV:C_NAV + 512].rearrange("(k p) d -> p k d", p=128), writes=['NV'])
        scale = 64.0 ** -0.5
        for h in range(8):
            q, qk = QT.next()
            k, kk = KT.next()
            e, ek = E.next()
            out, outk = OUT.next()
            P.dma('sp', q[0:64, :], self.zF[s][R_NAQ + h * 64:R_NAQ + (h + 1) * 64, :], writes=[qk])
            P.dma('sp', k[0:64, :], self.zF[s][R_NAK + h * 64:R_NAK + (h + 1) * 64, :], writes=[kk])
            P.dma('sp', e, self.naE[h].rearrange("p (j c) -> p j c", j=21), writes=[ek])
            for m in range(32):
                kts = na_kts(m)
                po, pok = psO.next()
                pu, puk = psU.next()

                def scores(kt):
                    pS, pSk = psS.next()
                    self.mm(pS, k[0:64, kt * 128:(kt + 1) * 128], q[0:64, m * 128:(m + 1) * 128], True, True, [kk, qk], [pSk])
                    es_, esk = ES.next()
                    self.act(es_, pS, AF.Exp, [pSk], [esk], scale=scale)
                    pt, ptk = PT.next()
                    self.tt('dve', pt, es_, e[:, na_case(m, kt), :], ALU.mult, [esk, ek], [ptk])
                    return pt, ptk
                cur = scores(kts[0])
                for j, kt in enumerate(kts):
                    nxt = scores(kts[j + 1]) if j + 1 < len(kts) else None
                    pt, ptk = cur
                    self.mm(po[0:64, :], V[:, kt, h * 64:(h + 1) * 64], pt, j == 0, j == len(kts) - 1, ['NV', ptk], [pok])
                    self.mm(pu[0:64, :], self.onesb[:, 0:64], pt, j == 0, j == len(kts) - 1, [ptk], [puk])
                    cur = nxt
                self.recip(rcp[0:64, :], pu[0:64, :], [puk], ['Nrcp'])
                self.tt('dve', out[0:64, m * 128:(m + 1) * 128], po[0:64, :], rcp[0:64, :], ALU.mult, [pok, 'Nrcp'], [outk])
            P.dma('pool', self.brT[s][0][h * 64:(h + 1) * 64, :], out[0:64, :], reads=[outk])

    def sin_rr(self, out, psum_in, bias_ap, np_, tmp_f, tmp_i, reads, writes, tag):
        two_pi = 2.0 * math.pi
        tf = tmp_f[0:np_, :]
        ti = tmp_i[0:np_, :]
        o = out[0:np_, :]
        fk, ik = tag + 'f', tag + 'i'
        self.ts('dve', tf, psum_in, bias_ap, ALU.add, reads, [fk], s2=1.0 / two_pi, op1=ALU.mult)
        self.ts('dve', ti, tf, 32.5, ALU.add, [fk], [ik])
        self.cp('dve', o, ti, [ik], writes)
        self.stt(o, tf, 32.0, o, ALU.add, ALU.subtract, [fk] + list(writes), writes)
        self.P.op('dve', lambda e: e.tensor_single_scalar(tf, o, -0.5, ALU.is_lt), list(writes) + [fk], [fk])
        self.tt('dve', o, o, tf, ALU.add, [fk] + list(writes), writes)
        self.ts('dve', o, o, two_pi, ALU.mult, writes, writes, s2=math.pi, op1=ALU.min)
        self.ts('dve', o, o, -math.pi, ALU.max, writes, writes)
        self.act(o, o, AF.Sin, writes, writes)

    def hy_filters(self, l):
        P = self.P
        self.arena_reset()
        I32 = mybir.dt.int32
        ZP = self.tile([128, T], F32)
        H1 = self.tile([128, T], F32)
        H2 = self.tile([128, T], F32)
        tf = self.tile([128, T], F32)
        ti = self.tile([128, T], I32)
        WIN = self.tile([128, 512], F32)
        HF = self.tile([128, 512], F32)
        HB = self.tile([128, 512], F32)
        AB = self.tile([128, 512], F32)
        KS = self.tile([128, 32, 512], BF16)
        KD = self.tile([128, 32, 512], BF16)
        rinv = self.tile([128, 512], F32)
        ps_abs = self.ps[7][:]
        w1 = self.PLv('hy_w1')
        w2 = self.PLv('hy_w2')
        w3 = self.PLv('hy_w3')
        b1 = self.PLv('hy_b1')
        b2 = self.PLv('hy_b2')
        nt = self.C('negt')
        for i in range(NT):
            t0 = i * T
            P.dma('sp', ZP[0:33, :], self.hyz_in[:, t0:t0 + T], writes=['FZP'])
            self.mm(self.ps[0][0:64, :], w1[0:33, :], ZP[0:33, :], True, True, ['FZP'], [('ps', 0)])
            self.sin_rr(H1, self.ps[0][0:64, :], b1[0:64, :], 64, tf, ti, [('ps', 0)], ['FH1'], 'Fa')
            self.mm(self.ps[1][0:64, :], w2[0:64, :], H1[0:64, :], True, True, ['FH1'], [('ps', 1)])
            self.sin_rr(H2, self.ps[1][0:64, :], b2[0:64, :], 64, tf, ti, [('ps', 1)], ['FH2'], 'Fb')
            for j in range(4):
                tix = i * 4 + j
                self.mm(self.ps[2][:], H2[0:64, j * 128:(j + 1) * 128], w3[0:64, 0:512], True, True, ['FH2'], [('ps', 2)])
                self.mm(self.ps[3][:], H2[0:64, j * 128:(j + 1) * 128], w3[0:64, 512:1024], True, True, ['FH2'], [('ps', 3)])
                self.act(WIN, self.C('deltas'), AF.Exp, [], ['FWIN'], scale=nt[:, tix:tix + 1])
                self.tt('dve', HF, self.ps[2][:], WIN, ALU.mult, [('ps', 2), 'FWIN'], ['FHF'])
                self.tt('dve', HB, self.ps[3][:], WIN, ALU.mult, [('ps', 3), 'FWIN'], ['FHB'])
                self.tt('pool', KS[:, tix, :], HF, HB, ALU.add, ['FHF', 'FHB'], ['FKS'])
                self.tt('pool', KD[:, tix, :], HF, HB, ALU.subtract, ['FHF', 'FHB'], ['FKD'])
                self.act(HF, HF, AF.Abs, ['FHF', 'FKS', 'FKD'], ['FHF'])
                self.act(HB, HB, AF.Abs, ['FHB', 'FKS', 'FKD'], ['FHB'])
                self.tt('dve', AB, HF, HB, ALU.add, ['FHF', 'FHB'], ['FAB'])
                self.mm(ps_abs, self.C('ones'), AB, tix == 0, tix == 31, ['FAB'], [('ps', 7)])
        self.ts('dve', rinv, ps_abs, RMS_EPS, ALU.add, [('ps', 7)], ['Frinv'])
        self.recip(rinv, rinv, ['Frinv'], ['Frinv'])
        ftr = Rot('Fft', [self.tile([128, 32, 128], BF16) for _ in range(2)])
        ny = self.tile([128, 32], BF16)
        kst = Rot('Fkst', [self.tile([128, 512], F32) for _ in range(2)])
        psr = self.psrot('Fps', range(0, 4))
        P.dma('sp', ny, self.dftny_in, writes=['Fny'])
        for rt in range(64):
            ft, ftk = ftr.next()
            P.dma('sp', ft, self.dftf_in[rt], writes=[ftk])
            pt, pk = psr.next()
            src = KS if rt < 32 else KD
            sk = 'FKS' if rt < 32 else 'FKD'
            for c in range(32):
                self.mm(pt, ft[:, c, :], src[:, c, :], c == 0, c == 31, [ftk, sk], [pk])
            st, stk = kst.next()
            self.tt('dve', st, pt, rinv, ALU.mult, [pk, 'Frinv'], [stk])
            if rt == 32:
                pn = self.ps[4][:]
                for c in range(32):
                    self.mm(pn[0:1, :], ny[:, c:c + 1], KS[:, c, :], c == 0, c == 31, ['Fny', 'FKS'], [('ps', 4)])
                self.tt('dve', st[0:1, :], pn[0:1, :], rinv[0:1, :], ALU.mult, [('ps', 4), 'Frinv', stk], [stk])
            P.dma('pool', self.hyKf[rt], st, reads=[stk])

    def hyena(self, s):
        P = self.P
        self.arena_reset()
        VX = self.tile([128, 32, 512], BF16)
        PA = self.tile([128, 64, 512], BF16)
        mark = self.aoff
        U = Rot('HU', [self.tile([128, 12, T + 2], BF16) for _ in range(2)])
        ucr = Rot('Huc', [self.tile([128, T], F32) for _ in range(3)])
        X1 = self.tile([128, 4, T], F32)
        vxf = Rot('Hvxf', [self.tile([128, T], F32) for _ in range(2)])
        vxb = self.tile([128, 4, T], BF16)
        x2r = Rot('Hx2', [self.tile([128, T], F32) for _ in range(2)])
        wsh = self.PLv('hy_short')
        zf = self.zF[s]
        uv = zf[R_HY:R_HY + 1536, :].rearrange("(c p) t -> p c t", p=128)
        psr = self.psrot('Hps', range(0, 4))
        for i in range(NT):
            t0 = i * T
            u, uk = U.next()
            lo = max(t0 - 1, 0)
            hi = min(t0 + T + 1, L)
            if i == 0:
                self.P.op('pool', lambda e, u=u: e.memset(u[:, :, 0:1], 0.0), [], [uk])
            if i == NT - 1:
                self.P.op('pool', lambda e, u=u: e.memset(u[:, :, T + 1:T + 2], 0.0), [], [uk])
            P.dma('sp', u[:, :, lo - (t0 - 1):hi - (t0 - 1)], uv[:, :, lo:hi], writes=[uk], reads=[uk])
            for c in range(12):
                a, ak = ucr.next()
                self.act(a, u[:, c, 0:T], AF.Copy, [uk], [ak], scale=wsh[:, c:c + 1])
                self.stt(a, u[:, c, 1:T + 1], wsh[:, 12 + c:13 + c], a, ALU.mult, ALU.add, [uk, ak], [ak])
                if c < 4:
                    self.stt(X1[:, c, :], u[:, c, 2:T + 2], wsh[:, 24 + c:25 + c], a, ALU.mult, ALU.add, [uk, ak], ['HX1'])
                elif c < 8:
                    x2, x2k = x2r.next()
                    self.stt(x2, u[:, c, 2:T + 2], wsh[:, 24 + c:25 + c], a, ALU.mult, ALU.add, [uk, ak], [x2k])
                    P.dma('pool', self.hyX2[s][(c - 4) * 128:(c - 3) * 128, t0:t0 + T], x2, reads=[x2k])
                else:
                    self.stt(a, u[:, c, 2:T + 2], wsh[:, 24 + c:25 + c], a, ALU.mult, ALU.add, [uk, ak], [ak])
                    vf, vfk = vxf.next()
                    self.tt('pool', vf, a, X1[:, c - 8, :], ALU.mult, [ak, 'HX1'], [vfk])
                    self.cp('pool', vxb[:, c - 8, :], vf, [vfk], ['Hvxb'])
                    P.dma('pool', self.hyVX[s][(c - 8) * 128:(c - 7) * 128, t0:t0 + T], vf, reads=[vfk])
            for j in range(4):
                pt, pk = psr.next()
                ptb = pt.bitcast(BF16)
                for c in range(4):
                    self.tr(ptb[:, c * 128:(c + 1) * 128], vxb[:, c, j * 128:(j + 1) * 128], self.identb[:], ['Hvxb'], [pk])
                self.cp('act', VX[:, i * 4 + j, :], ptb[:, 0:512], [pk], ['HVX'])
        P.barrier()
        self.aoff = mark
        ftr = Rot('Hft', [self.tile([128, 32, 128], BF16) for _ in range(3)])
        kfr = Rot('Hkf', [self.tile([128, 2, 512], F32) for _ in range(2)])
        tm = [self.tile([128, 512], F32) for _ in range(4)]
        psr = self.psrot('Hps2', range(0, 4))
        for j in range(32):
            kf, kfk = kfr.next()
            P.dma('sp', kf[:, 0, :], self.hyKf[j], writes=[kfk])
            P.dma('sp', kf[:, 1, :], self.hyKf[j + 32], writes=[kfk], reads=[kfk])
            pcs = []
            for half in range(2):
                ft, ftk = ftr.next()
                P.dma('sp', ft, self.dftf_in[j + 32 * half], writes=[ftk])
                pt, pk = psr.next()
                for c in range(32):
                    self.mm(pt, ft[:, c, :], VX[:, c, :], c == 0, c == 31, [ftk, 'HVX'], [pk])
                pcs.append((pt, pk))
            (pc, pck), (ps_, psk) = pcs
            self.tt('dve', tm[0], pc, kf[:, 0, :], ALU.mult, [pck, kfk], ['Ht0'])
            self.tt('dve', tm[1], ps_, kf[:, 1, :], ALU.mult, [psk, kfk], ['Ht1'])
            self.tt('pool', PA[:, j, :], tm[0], tm[1], ALU.subtract, ['Ht0', 'Ht1'], ['HPA'])
            self.tt('dve', tm[2], pc, kf[:, 1, :], ALU.mult, [pck, kfk], ['Ht2'])
            self.tt('dve', tm[3], ps_, kf[:, 0, :], ALU.mult, [psk, kfk], ['Ht3'])
            self.tt('pool', PA[:, j + 32, :], tm[2], tm[3], ALU.add, ['Ht2', 'Ht3'], ['HPA'])
            if j == 0:
                self.tt('dve', PA[0:1, 0, :], pc[0:1, :], kf[0:1, 0, :], ALU.mult, [pck, kfk, 'HPA'], ['HPA2'])
                self.tt('dve', PA[0:1, 32, :], ps_[0:1, :], kf[0:1, 1, :], ALU.mult, [psk, kfk, 'HPA', 'HPA2'], ['HPA2'])
        P.barrier()
        self.aoff = mark
        ivr = Rot('Hiv', [self.tile([128, 16, 512], BF16) for _ in range(2)])
        vxr = Rot('Hvx', [self.tile([128, T], F32) for _ in range(2)])
        x2r = Rot('Hx2b', [self.tile([128, T], F32) for _ in range(2)])
        yr = Rot('Hy', [self.tile([128, T], F32) for _ in range(2)])
        yo = Rot('Hyo', [self.tile([128, T], BF16) for _ in range(2)])
        skip = self.PLv('hy_skip')
        for i in range(NT):
            t0 = i * T
            pacc = [(self.ps[(i % 2) * 4 + c][:], ('ps', (i % 2) * 4 + c)) for c in range(4)]
            for g in range(4):
                iv, ivk = ivr.next()
                P.dma('sp', iv, self.dfti_in[i][g], writes=[ivk])
                for ct in range(4):
                    pt, pk = pacc[ct]
                    for c in range(16):
                        self.mm(pt, PA[:, g * 16 + c, ct * 128:(ct + 1) * 128], iv[:, c, :],
                                g == 0 and c == 0, g == 3 and c == 15, [ivk], [pk])
            for ct in range(4):
                pt, pk = pacc[ct]
                vx, vxk = vxr.next()
                x2, x2k = x2r.next()
                P.dma('sp', vx, self.hyVX[s][ct * 128:(ct + 1) * 128, t0:t0 + T], writes=[vxk])
                P.dma('sp', x2, self.hyX2[s][ct * 128:(ct + 1) * 128, t0:t0 + T], writes=[x2k])
                y, yk = yr.next()
                self.stt(y, vx, skip[:, ct:ct + 1], pt, ALU.mult, ALU.add, [vxk, pk], [yk])
                o, ok_ = yo.next()
                self.tt('pool', o, y, x2, ALU.mult, [yk, x2k], [ok_])
                P.dma('pool', self.brT[s][1][ct * 128:(ct + 1) * 128, t0:t0 + T], o, reads=[ok_])

    def deltanet(self, s, l):
        P = self.P
        self.arena_reset()
        import os as _os
        sub = (lambda x: self.phases is None or 'D' in self.phases or x in self.phases)
        dn_steps = int(_os.environ.get('DN_STEPS', '64'))
        dn_mode = float(_os.environ.get('DN_MODE', '9'))
        A = [Rot('DA%d' % k, [self.tile([128, 1536], BF16) for _ in range(2)]) for k in range(3)]
        cf = self.tile([128, 1536], F32)
        c2 = self.tile([128, 1536], F32)
        sq = self.tile([128, 1024], F32)
        ss = self.tile([128, 8], F32)
        ob = Rot('Dob', [self.tile([128, 1536], BF16) for _ in range(2)])
        abt = self.tile([128, 16], F32)
        gb = Rot('Dgb', [self.tile([128, 16], F32) for _ in range(2)])
        gb2 = Rot('Dgb2', [self.tile([128, 16], F32) for _ in range(2)])
        nA = self.tile([128, 8], F32)
        wc = self.tile([128, 3 * 1536], F32)
        P.dma('sp', wc, self.dnconv_in[l].rearrange("k c -> (k c)").partition_broadcast(128), writes=['Dwc'])
        P.barrier()
        zt = self.zT[s]
        self.act(nA, self.PLv('dn_alog'), AF.Exp, [], ['DnA'])
        self.ts('dve', nA, nA, -1.0, ALU.mult, ['DnA'], ['DnA'])
        for j in range(32 if sub('D1') else 0):
            t0 = j * 128
            tiles = []
            for k in range(3):
                a, ak = A[k].next()
                sh = k - 1
                lo = max(t0 + sh, 0)
                hi = min(t0 + sh + 128, L)
                if lo != t0 + sh or hi != t0 + sh + 128:
                    self.P.op('pool', lambda e, a=a: e.memset(a, 0.0), [], [ak])
                P.dma('sp', a[lo - (t0 + sh):hi - (t0 + sh), :], zt[lo:hi, C_DNQ:C_DNQ + 1536], writes=[ak], reads=[ak])
                tiles.append((a, ak))
            (a0, a0k), (a1, a1k), (a2, a2k) = tiles
            self.tt('dve', c2, a0, wc[:, 0:1536], ALU.mult, [a0k], ['Dc2'])
            self.tt('dve', cf, a1, wc[:, 1536:3072], ALU.mult, [a1k], ['Dcf'])
            self.tt('dve', cf, cf, c2, ALU.add, ['Dcf', 'Dc2'], ['Dcf'])
            self.tt('dve', c2, a2, wc[:, 3072:4608], ALU.mult, [a2k, 'Dcf'], ['Dc2'])
            self.tt('dve', cf, cf, c2, ALU.add, ['Dcf', 'Dc2'], ['Dcf'])
            self.act(cf, cf, AF.Silu, ['Dcf'], ['Dcf'])
            self.tt('pool', sq, cf[:, 0:1024], cf[:, 0:1024], ALU.mult, ['Dcf'], ['Dsq'])
            self.reduce_x(ss, sq.rearrange("p (h d) -> p h d", h=8), ['Dsq'], ['Dss'])
            self.act(ss, ss, AF.Sqrt, ['Dss'], ['Dss'], bias=RMS_EPS, scale=1.0)
            self.recip(ss, ss, ['Dss'], ['Dss'])
            o, okk = ob.next()
            for hh in range(8):
                if hh < 4:
                    self.ts('dve', o[:, hh * 128:(hh + 1) * 128], cf[:, hh * 128:(hh + 1) * 128], ss[:, hh:hh + 1], ALU.mult,
                            ['Dcf', 'Dss'], [okk], s2=128.0 ** -0.5, op1=ALU.mult)
                else:
                    self.ts('dve', o[:, hh * 128:(hh + 1) * 128], cf[:, hh * 128:(hh + 1) * 128], ss[:, hh:hh + 1], ALU.mult,
                            ['Dcf', 'Dss'], [okk])
            self.cp('act', o[:, 1024:1536], cf[:, 1024:1536], ['Dcf'], [okk])
            P.dma('pool', self.dnQ[s][t0:t0 + 128, :], o[:, 0:512], reads=[okk])
            P.dma('pool', self.dnK[s][t0:t0 + 128, :], o[:, 512:1024], reads=[okk])
            P.dma('pool', self.dnV[s][t0:t0 + 128, :], o[:, 1024:1536], reads=[okk])
            P.dma('sp', abt, self.zAB[s][t0:t0 + 128, :], writes=['Dab'])
            g, gk = gb.next()
            g2, g2k = gb2.next()
            abv = abt.rearrange("p (d w h) -> p d w h", d=2, w=2)
            gv = g.rearrange("p (w d h) -> p w d h", w=2, d=2)
            dtb = self.PLv('dn_dtb').rearrange("p (d h) -> p d h", d=2)
            nAv = nA.rearrange("p (d h) -> p d h", d=2)
            self.tt('dve', gv[:, 0, :, :], abv[:, :, 0, :], dtb, ALU.add, ['Dab'], [gk])
            self.act(g[:, 0:8], g[:, 0:8], AF.Exp, [gk], [gk])
            self.act(g[:, 0:8], g[:, 0:8], AF.Ln, [gk], [gk], bias=1.0, scale=1.0)
            self.tt('dve', gv[:, 0, :, :], gv[:, 0, :, :], nAv, ALU.mult, [gk, 'DnA'], [gk])
            self.act(gv[:, 1, :, :], abv[:, :, 1, :], AF.Sigmoid, ['Dab', gk], [gk])
            for w_ in range(2):
                self.cp('dve', g2.rearrange("p (d h2 w hp) -> p w d hp h2", d=2, h2=2, w=2)[:, w_],
                        g.rearrange("p (w d hp h2) -> p w d hp h2", w=2, d=2, hp=2)[:, w_], [gk], [g2k])
            P.dma('pool', self.dnGB[s][t0:t0 + 128, :], g2, reads=[g2k])
        P.barrier()
        self.arena_reset()
        ident = self.C('ident')
        Sst = [[self.tile([128, 128], F32) for _ in range(4)] for _ in range(2)]
        S16 = [[self.tile([128, 128], BF16) for _ in range(4)] for _ in range(2)]
        for d in range(2):
            for h in range(4):
                self.P.op('pool', lambda e, t=Sst[d][h]: e.memset(t, 0.0), [], [('DS', d, h)])
                self.P.op('pool', lambda e, t=S16[d][h]: e.memset(t, 0.0), [], [('DS16', d, h)])
        NB = 6
        mk = lambda shape, dt: [self.tile(shape, dt) for _ in range(NB)]
        QK = [mk([128, 2, 128], BF16) for _ in range(3)]
        GBt = mk([128, 8], F32)
        gc = mk([128, 2], F32)
        gt = mk([128, 2], F32)
        sc = mk([128, 8], F32)
        egl = mk([128, 4], F32)
        scaled = mk([128, 5, 128], BF16)
        FMt = mk([128, 4, 128], BF16)
        Dg = mk([128, 128], F32)
        dm = mk([128, 128], F32)
        dB = mk([128, 128], F32)
        dBi = mk([128, 128], F32)
        Nm = mk([128, 128], F32)
        Nt = mk([128, 128], F32)
        Q2 = mk([128, 128], F32)
        Qt2 = mk([128, 128], F32)
        Pm = mk([128, 128], F32)
        TT = mk([128, 128], BF16)
        Aqk = mk([128, 128], BF16)
        um = mk([128, 128], F32)
        wT = mk([128, 128], BF16)
        vn = mk([128, 128], BF16)
        ot = mk([128, 128], F32)
        cnt = [0]
        psq = self.psrot('Dps', range(8))

        def P128(p):
            return p[:, 0:128]
        for step in range(dn_steps if sub('D2') else 0):
            for d in range(2):
                n = step if d == 0 else 63 - step
                t0 = n * 64
                for hp in range(2):
                    b = cnt[0] % NB
                    cnt[0] += 1
                    K = lambda nm: ('D' + nm, b)
                    q_, k_, v_ = QK[0][b], QK[1][b], QK[2][b]
                    for h2 in range(2):
                        for (dst, src, nm) in ((q_, self.dnQ, 'q'), (k_, self.dnK, 'k'), (v_, self.dnV, 'v')):
                            P.dma('sp', dst[h2 * 64:(h2 + 1) * 64, :, :],
                                  src[s][t0:t0 + 64, :].rearrange("t (hp h2 e) -> t hp h2 e", hp=2, h2=2)[:, :, h2, :],
                                  writes=[K(nm)], reads=[K(nm)])
                        gsrc = self.dnGB[s][t0:t0 + 64, :].rearrange("t (d h2 x) -> t d h2 x", d=2, h2=2)
                        P.dma('sp', GBt[b][h2 * 64:(h2 + 1) * 64, 0:4], gsrc[:, d, h2, :], writes=[K('gb')], reads=[K('gb')])
                    g_col = GBt[b][:, hp:hp + 1]
                    be_col = GBt[b][:, 2 + hp:3 + hp]
                    tri = self.C('tri%d' % d)
                    p1, p1k = psq.next()
                    self.mm(p1[:, 0:2], tri, GBt[b][:, 0:2], True, True, [K('gb')], [p1k])
                    self.mm(p1[:, 2:4], self.C('onesblk'), GBt[b][:, 0:2], True, True, [K('gb')], [p1k])
                    self.mm(p1[:, 4:6], self.C('sel0'), GBt[b][:, 0:2], True, True, [K('gb')], [p1k])
                    self.mm(p1[:, 6:8], self.C('sel1'), GBt[b][:, 0:2], True, True, [K('gb')], [p1k])
                    self.cp('dve', gc[b], p1[:, 0:2], [p1k], [K('gc')])
                    self.cp('dve', gt[b], p1[:, 2:4], [p1k], [K('gt')])
                    self.act(egl[b], p1[:, 4:8], AF.Exp, [p1k], [K('egl')])
                    gcc = gc[b][:, hp:hp + 1]
                    self.act(sc[b][:, 0:1], gcc, AF.Exp, [K('gc')], [K('sc')])
                    self.tt('dve', sc[b][:, 1:2], gt[b][:, hp:hp + 1], gcc, ALU.subtract, [K('gt'), K('gc'), K('sc')], [K('sc')])
                    self.act(sc[b][:, 1:2], sc[b][:, 1:2], AF.Exp, [K('sc')], [K('sc')])
                    self.tt('dve', sc[b][:, 2:3], sc[b][:, 0:1], be_col, ALU.mult, [K('sc'), K('gb')], [K('sc')])
                    kb, kbg, kd, qd, vb = [scaled[b][:, x, :] for x in range(5)]
                    kk_ = k_[:, hp, :]
                    self.ts('dve', kb, kk_, be_col, ALU.mult, [K('k'), K('gb')], [K('kb')])
                    self.ts('dve', kbg, kk_, sc[b][:, 2:3], ALU.mult, [K('k'), K('sc')], [K('kbg')])
                    self.ts('dve', kd, kk_, sc[b][:, 1:2], ALU.mult, [K('k'), K('sc')], [K('kd')])
                    self.ts('dve', qd, q_[:, hp, :], sc[b][:, 0:1], ALU.mult, [K('q'), K('sc')], [K('qd')])
                    self.ts('dve', vb, v_[:, hp, :], be_col, ALU.mult, [K('v'), K('gb')], [K('vb')])
                    if dn_mode < 2:
                        continue
                    p2, p2k = psq.next()
                    p2b = p2.bitcast(BF16)
                    self.tr(p2b[:, 0:128], kk_, self.identb[:], [K('k')], [p2k])
                    self.tr(p2b[:, 128:256], kb, self.identb[:], [K('kb')], [p2k])
                    self.tr(p2b[:, 256:384], q_[:, hp, :], self.identb[:], [K('q')], [p2k])
                    self.tr(p2b[:, 384:512], qd, self.identb[:], [K('qd')], [p2k])
                    fm = FMt[b]
                    self.cp('act', fm.rearrange("p a b -> p (a b)"), p2b[:, 0:512], [p2k], [K('fm')])
                    kT, kbT, qsT, qdT = [fm[:, x, :] for x in range(4)]
                    if dn_mode < 2.2:
                        continue
                    p3, p3k = psq.next()
                    self.mm(p3[:, 0:128], kT, kbT, True, True, [K('fm')], [p3k])
                    self.mm(p3[:, 128:256], kT, qsT, True, True, [K('fm')], [p3k])
                    if dn_mode < 2.4:
                        continue
                    self.ts('dve', Dg[b], ident, gcc, ALU.mult, [K('gc')], [K('Dg')])
                    self.mm(p3[:, 256:384], self.C('onesblk'), Dg[b], True, True, [K('Dg')], [p3k])
                    if dn_mode < 2.6:
                        continue
                    self.ts('dve', dm[b], p3[:, 256:384], gcc, ALU.subtract, [p3k, K('gc')], [K('dm')], s2=0.0, op1=ALU.min)
                    self.act(dm[b], dm[b], AF.Exp, [K('dm')], [K('dm')])
                    if dn_mode < 2.8:
                        continue
                    self.tt('pool', dB[b], dm[b], self.C('maskB%d' % d), ALU.mult, [K('dm')], [K('dB')])
                    self.tt('pool', dBi[b], dm[b], self.C('maskBi%d' % d), ALU.mult, [K('dm')], [K('dBi')])
                    self.stt(Nm[b], p3[:, 0:128], -1.0, dB[b], ALU.mult, ALU.mult, [p3k, K('dB')], [K('N')])
                    self.tt('dve', Aqk[b], p3[:, 128:256], dBi[b], ALU.mult, [p3k, K('dBi')], [K('Aqk')])
                    if dn_mode < 3:
                        continue
                    p4, p4k = psq.next()
                    self.tr(p4[:, 0:128], Nm[b], ident, [K('N')], [p4k])
                    self.cp('act', Nt[b], p4[:, 0:128], [p4k], [K('Nt')])
                    if dn_mode < 3.2:
                        continue
                    self.tt('pool', Pm[b], Nm[b], ident, ALU.add, [K('N')], [K('P')])
                    if dn_mode < 3.4:
                        continue
                    Q, Qk = Nm[b], K('N')
                    Qt, Qtk = Nt[b], K('Nt')
                    Qn, Qnk = Q2[b], K('Q2')
                    Qtn, Qtnk = Qt2[b], K('Qt2')
                    for it in range(5):
                        pa, pak = psq.next()
                        self.mm(pa[:, 0:128], Q, Qt, True, True, [Qk, Qtk], [pak])
                        if it < 4:
                            self.mm(pa[:, 128:256], Qt, Q, True, True, [Qk, Qtk], [pak])
                        self.cp('act', Qtn, pa[:, 0:128], [pak], [Qtnk])
                        if it < 4:
                            self.cp('dve', Qn, pa[:, 128:256], [pak], [Qnk])
                        pb_, pbk = psq.next()
                        self.mm(pb_[:, 0:128], Qtn, Pm[b], True, True, [Qtnk, K('P')], [pbk])
                        if it < 4:
                            self.tt('dve', Pm[b], Pm[b], pb_[:, 0:128], ALU.add, [pbk, K('P')], [K('P')])
                        else:
                            self.tt('dve', TT[b], Pm[b], pb_[:, 0:128], ALU.add, [pbk, K('P')], [K('TT')])
                        Q, Qk, Qn, Qnk = Qn, Qnk, Q, Qk
                        Qt, Qtk, Qtn, Qtnk = Qtn, Qtnk, Qt, Qtk
                    if dn_mode < 4:
                        continue
                    p5, p5k = psq.next()
                    self.mm(p5[:, 0:128], TT[b], vb, True, True, [K('TT'), K('vb')], [p5k])
                    self.mm(p5[:, 128:256], kbg, TT[b], True, True, [K('TT'), K('kbg')], [p5k])
                    self.cp('act', um[b], p5[:, 0:128], [p5k], [K('u')])
                    self.cp('dve', wT[b], p5[:, 128:256], [p5k], [K('wT')])
                    if dn_mode < 5:
                        continue
                    hs = [2 * hp + h2 for h2 in range(2)]
                    p6, p6k = psq.next()
                    for h2 in range(2):
                        h = hs[h2]
                        self.mm(p6[h2 * 64:(h2 + 1) * 64, 0:128], wT[b][:, h2 * 64:(h2 + 1) * 64], S16[d][h], True, True,
                                [K('wT'), ('DS16', d, h)], [p6k])
                    self.tt('dve', vn[b], um[b], p6[:, 0:128], ALU.subtract, [K('u'), p6k], [K('vn')])
                    if dn_mode < 6:
                        continue
                    p7, p7k = psq.next()
                    for h2 in range(2):
                        h = hs[h2]
                        self.mm(p7[h2 * 64:(h2 + 1) * 64, 0:128], qdT[:, h2 * 64:(h2 + 1) * 64], S16[d][h], True, False,
                                [K('fm'), ('DS16', d, h)], [p7k])
                    self.mm(p7[:, 0:128], Aqk[b], vn[b], False, True, [K('Aqk'), K('vn')], [p7k])
                    self.cp('act', ot[b], p7[:, 0:128], [p7k], [K('o')])
                    for h2 in range(2):
                        h = hs[h2]
                        P.dma('pool', self.dnO[s][d][t0:t0 + 64, h * 128:(h + 1) * 128], ot[b][h2 * 64:(h2 + 1) * 64, :], reads=[K('o')])
                    if dn_mode < 7:
                        continue
                    for h2 in range(2):
                        h = hs[h2]
                        p8, p8k = psq.next()
                        self.mm(p8[:, 0:128], kd[h2 * 64:(h2 + 1) * 64, :], vn[b][h2 * 64:(h2 + 1) * 64, :], True, True,
                                [K('kd'), K('vn')], [p8k])
                        self.stt(Sst[d][h], Sst[d][h], egl[b][:, h2 * 2 + hp:h2 * 2 + hp + 1], p8[:, 0:128],
                                 ALU.mult, ALU.add, [('DS', d, h), K('egl'), p8k], [('DS', d, h)])
                        self.cp('act', S16[d][h], Sst[d][h], [('DS', d, h)], [('DS16', d, h)])
        P.barrier()
        self.arena_reset()
        of = Rot('Dof', [self.tile([128, 512], F32) for _ in range(2)])
        obw = Rot('Dobw', [self.tile([128, 512], F32) for _ in range(2)])
        gt_ = Rot('Dgt', [self.tile([128, 512], BF16) for _ in range(2)])
        gs = self.tile([128, 512], F32)
        sq = self.tile([128, 512], F32)
        ss = self.tile([128, 4], F32)
        on = self.tile([128, 512], BF16)
        stg = Rot('Dstg', [self.tile([128, 4, 512], BF16) for _ in range(2)])
        psr = self.psrot('Dps3', range(4))
        gn = self.PLv('dn_gn')
        for j in range(32 if sub('D3') else 0):
            t0 = j * 128
            a, ak = of.next()
            bb, bk = obw.next()
            g, gk = gt_.next()
            P.dma('sp', a, self.dnO[s][0][t0:t0 + 128, :], writes=[ak])
            P.dma('sp', bb, self.dnO[s][1][t0:t0 + 128, :], writes=[bk])
            P.dma('sp', g, self.zT[s][t0:t0 + 128, C_DNG:C_DNG + 512], writes=[gk])
            self.tt('dve', a, a, bb, ALU.add, [ak, bk], [ak])
            self.tt('pool', sq, a, a, ALU.mult, [ak], ['D3sq'])
            self.reduce_x(ss, sq.rearrange("p (h d) -> p h d", h=4), ['D3sq'], ['D3ss'])
            self.act(ss, ss, AF.Sqrt, ['D3ss'], ['D3ss'], bias=RMS_EPS, scale=1.0 / 128)
            self.recip(ss, ss, ['D3ss'], ['D3ss'])
            self.act(gs, g, AF.Silu, [gk], ['D3gs'])
            self.tt('pool', gs, gs, gn, ALU.mult, ['D3gs'], ['D3gs'])
            for h in range(4):
                self.stt(on[:, h * 128:(h + 1) * 128], a[:, h * 128:(h + 1) * 128], ss[:, h:h + 1], gs[:, h * 128:(h + 1) * 128],
                         ALU.mult, ALU.mult, [ak, 'D3ss', 'D3gs'], ['D3on'])
            if j % 4 == 0:
                st, stk = stg.next()
            pt, pk = psr.next()
            ptb = pt.bitcast(BF16)
            for h in range(4):
                self.tr(ptb[:, h * 128:(h + 1) * 128], on[:, h * 128:(h + 1) * 128], self.identb[:], ['D3on'], [pk])
            self.cp('act', st[:, :, (j % 4) * 128:(j % 4 + 1) * 128], ptb[:, 0:512].rearrange("p (h t) -> p h t", h=4), [pk], [stk])
            if j % 4 == 3:
                tt0 = (j // 4) * 512
                P.dma('pool', self.brT[s][3][:, tt0:tt0 + 512].rearrange("(h p) t -> p h t", p=128), st, reads=[stk])

    def phase_C(self, s, l, xsrc, xdst, last):
        P = self.P
        self.arena_reset()
        X32 = self.tile([128, 16, T], F32)
        H = self.tile([128, 16, T], BF16)
        ob_ = self.carve(32 * 1024)
        bigf = self.arena[:, ob_:ob_ + 8192]
        big = bigf.bitcast(BF16).rearrange("p (a b) -> p a b", a=32)
        Yv = bigf.rearrange("p (a b) -> p a b", a=16)
        BR = big[:, 0:16, :]
        MG = big[:, 16:32, :]
        wrot = Rot('Cw', [self.tile([128, 16, 512], BF16) for _ in range(2)])
        wbrot = Rot('Cwb', [self.tile([128, 4, 512], BF16) for _ in range(2)])
        sig = Rot('Csig', [self.tile([128, T], F32) for _ in range(2)])
        tmp = Rot('Ctmp', [self.tile([128, T], F32) for _ in range(2)])
        acc = [self.tile([128, T], F32) for _ in range(4)]
        rl = Rot('Crl', [self.tile([128, T], F32) for _ in range(2)])
        xv = xsrc[s].rearrange("(c p) t -> p c t", p=128)
        hv = self.hT[s].rearrange("(c p) t -> p c t", p=128)
        ov = xdst[s].rearrange("(c p) t -> p c t", p=128)
        brv = self.brT[s].rearrange("n (c p) t -> p (n c) t", p=128)
        wGv = self.wG.rearrange("(c p) n -> p c n", p=128)
        wBv = self.wB.rearrange("n (c p) d -> p n c d", p=128)
        wOv = self.wO.rearrange("(c p) n -> p c n", p=128)
        wUv = self.wU.rearrange("(c p) n -> p c n", p=128)
        wDv = self.wD.rearrange("(c p) n -> p c n", p=128)
        psG = self.psrot('CpsG', range(0, 3))
        psP = self.psrot('CpsP', range(3, 6))
        mark = self.aoff
        for i in range(NT):
            t0 = i * T
            self.aoff = mark
            P.dma('sp', X32, xv[:, :, t0:t0 + T], writes=['CX'])
            P.dma('sp', H, hv[:, :, t0:t0 + T], writes=['CH'])
            P.dma('sp', BR, brv[:, :, t0:t0 + T], writes=['CBR', 'CHID', 'CY'])
            for mg in range(4):
                for n in range(4):
                    wt, wk = wrot.next()
                    wb, wbk = wbrot.next()
                    P.dma('sp', wt, wGv[:, :, n * 2048 + mg * 512:n * 2048 + (mg + 1) * 512], writes=[wk])
                    P.dma('sp', wb, wBv[:, n, :, mg * 512:(mg + 1) * 512], writes=[wbk])
                    for mt in range(4):
                        pg, pgk = psG.next()
                        for c in range(16):
                            self.mm(pg, wt[:, c, mt * 128:(mt + 1) * 128], H[:, c, :], c == 0, c == 15, [wk, 'CH'], [pgk])
                        pp, ppk = psP.next()
                        for c in range(4):
                            self.mm(pp, wb[:, c, mt * 128:(mt + 1) * 128], BR[:, n * 4 + c, :], c == 0, c == 3, [wbk, 'CBR'], [ppk])
                        sg, sgk = sig.next()
                        self.act(sg, pg, AF.Sigmoid, [pgk], [sgk])
                        ak = ('Cacc', mt)
                        if n == 0:
                            self.tt('dve', acc[mt], sg, pp, ALU.mult, [sgk, ppk], [ak])
                        else:
                            tp, tpk = tmp.next()
                            self.tt('dve', tp, sg, pp, ALU.mult, [sgk, ppk], [tpk])
                            if n < 3:
                                self.tt('pool', acc[mt], acc[mt], tp, ALU.add, [tpk, ak], [ak])
                            else:
                                self.tt('pool', MG[:, mg * 4 + mt, :], acc[mt], tp, ALU.add, [tpk, ak], ['CMG'])
            self.pump_casts(4)
            for mg in range(4):
                wt, wk = wrot.next()
                P.dma('sp', wt, wOv[:, :, mg * 512:(mg + 1) * 512], writes=[wk])
                for mt in range(4):
                    pg, pgk = psG.next()
                    for c in range(16):
                        self.mm(pg, wt[:, c, mt * 128:(mt + 1) * 128], MG[:, c, :], c == 0, c == 15, [wk, 'CMG'], [pgk])
                    ch = mg * 4 + mt
                    self.tt('dve', X32[:, ch, :], X32[:, ch, :], pg, ALU.add, [pgk, 'CX'], ['CX'])
            self.rms_fm(lambda c: X32[:, c, :], 16, D, self.PLv('g_mlp'), H, 7, 'C', ['CX'], 'CH')
            for half in range(2):
                for ug in range(8):
                    ugg = half * 8 + ug
                    wt, wk = wrot.next()
                    P.dma('sp', wt, wUv[:, :, ugg * 512:(ugg + 1) * 512], writes=[wk])
                    for mt in range(4):
                        pg, pgk = psG.next()
                        for c in range(16):
                            self.mm(pg, wt[:, c, mt * 128:(mt + 1) * 128], H[:, c, :], c == 0, c == 15, [wk, 'CH'], [pgk])
                        r, rk = rl.next()
                        self.act(r, pg, AF.Relu, [pgk], [rk])
                        self.tt('pool', big[:, ug * 4 + mt, :], r, r, ALU.mult, [rk], ['CHID', 'CBR', 'CMG'])
                for ng in range(4):
                    pacc = [(self.ps[3 + mt][:], ('ps', 3 + mt)) for mt in range(4)]
                    for kg in range(2):
                        kgg = half * 2 + kg
                        wt, wk = wrot.next()
                        P.dma('sp', wt, wDv[:, kgg * 16:(kgg + 1) * 16, ng * 512:(ng + 1) * 512], writes=[wk])
                        for mt in range(4):
                            pt, pk = pacc[mt]
                            for c in range(16):
                                self.mm(pt, wt[:, c, mt * 128:(mt + 1) * 128], big[:, kg * 16 + c, :],
                                        kg == 0 and c == 0, kg == 1 and c == 15, [wk, 'CHID'], [pk])
                    for mt in range(4):
                        pt, pk = pacc[mt]
                        ch = ng * 4 + mt
                        self.tt('dve', X32[:, ch, :], X32[:, ch, :], pt, ALU.add, [pk, 'CX'], ['CX'])
            if not last:
                P.dma('pool', ov[:, :, t0:t0 + T], X32, reads=['CX'])
            else:
                Y = Yv
                sqr = Rot('Cfsq', [self.tile([128, T], F32) for _ in range(2)])
                sd = self.tile([128, T], F32)
                pss = self.ps[7][:]
                for c in range(16):
                    sq, k = sqr.next()
                    self.act(sq, X32[:, c, :], AF.Square, ['CX'], [k])
                    self.mm(pss, self.C('ones'), sq, c == 0, c == 15, [k], [('ps', 7)])
                self.act(sd, pss, AF.Sqrt, [('ps', 7)], ['Cfsd'], bias=RMS_EPS, scale=1.0 / D)
                self.recip(sd, sd, ['Cfsd'], ['Cfsd'])
                for c in range(16):
                    self.stt(Y[:, c, :], X32[:, c, :], self.gfin[:, c:c + 1], sd, ALU.mult, ALU.mult, ['CX', 'Cfsd'], ['CY', 'CHID'])
                o = P.dma('pool', ov[:, :, t0:t0 + T], Y, reads=['CY'])
                self.finals.append(o)


_NC_CACHE = {}


def host_inputs_common(inp):
    hc = host_constants()
    drow, dcol = hc['na_idx']
    rpb = np.asarray(inp['na_rpb'], np.float32)
    g = rpb[:, :, drow, dcol]
    rpbT = np.ascontiguousarray(g.transpose(0, 1, 3, 2, 4).reshape(DEPTH, 8, 128, 21 * 128))
    pl = np.stack([pack_layer_params(inp, l) for l in range(DEPTH)], 0)
    common = {
        "w_in": np.ascontiguousarray(inp['w_in'], dtype=np.float32),
        "mla_w_uq": np.ascontiguousarray(inp['mla_w_uq'], dtype=np.float32),
        "mla_w_ukv": np.ascontiguousarray(inp['mla_w_ukv'], dtype=np.float32),
        "w_branch": np.ascontiguousarray(inp['w_branch'], dtype=np.float32),
        "w_out": np.ascontiguousarray(inp['w_out'], dtype=np.float32),
        "w_up": np.ascontiguousarray(inp['w_up'], dtype=np.float32),
        "w_down": np.ascontiguousarray(inp['w_down'], dtype=np.float32),
        "pl": pl,
        "gfin": np.ascontiguousarray(np.asarray(inp['norm_final'], np.float32).reshape(16, 128).T),
        "cp": hc['cp'],
        "rope": hc['rope'],
        "hyz": hc['hyz'],
        "dftf": hc['dftf'],
        "dftny": hc['dftny'],
        "dfti": hc['dfti'],
        "rpbT": rpbT,
        "namask": hc['namask'],
        "dn_conv": np.ascontiguousarray(inp['dn_conv'], dtype=np.float32),
    }
    return common


def kernel(**inputs):
    inp = {k: np.asarray(v) for k, v in inputs.items()}
    xp = inp['x_prompt'].astype(np.float32, copy=False)
    xs = inp['x_sample'].astype(np.float32, copy=False)
    ncores = 8
    common = host_inputs_common(inp)
    in_maps = []
    for c in range(ncores):
        second = xs[c] if c < xs.shape[0] else xs[c % xs.shape[0]]
        xT = np.ascontiguousarray(np.stack([xp[c].T, second.T], 0))
        m = dict(common)
        m["xT"] = xT
        in_maps.append(m)
    if 'nc' not in _NC_CACHE:
        _NC_CACHE['nc'] = Builder(nseq=2).build()
    nc = _NC_CACHE['nc']
    res = run_bass_kernel_spmd(nc, in_maps, core_ids=list(range(ncores)))
    outs = [r["yT"] for r in res.results]
    y_prompt = np.ascontiguousarray(np.stack([outs[c][0].T for c in range(8)], 0))
    y_sample = np.ascontiguousarray(np.stack([outs[c][1].T for c in range(2)], 0))
    return (y_prompt, y_sample)
```
